# Optimizing a Trainium2 kernel written in Bass

```python
import math
import jax, jax.numpy as jnp
from jax import lax
import numpy as np

D_MODEL = 1024
BATCH = 8
SEQ = 8192
DEPTH = 2

GDN_HEADS = D_MODEL // 128
GDN_DK = 128
GDN_DV = 256
GDN_QK_W = GDN_HEADS * GDN_DK
GDN_V_W = GDN_HEADS * GDN_DV
GDN_CONV_W = 2 * GDN_QK_W + GDN_V_W
GDN_IN_W = GDN_CONV_W + GDN_V_W + 2 * GDN_HEADS
CONV_K = 4
CHUNK = 64
DIL_GROUPS = ((128, 1), (512, 4), (2048, 16))
N_GROUPS = 3
HEADS_PER_GROUP = 4
HEAD_DIM = 128
ATT_Q_W = N_GROUPS * HEADS_PER_GROUP * HEAD_DIM
ATT_OUT_W = HEADS_PER_GROUP * HEAD_DIM
ATT_IN_W = ATT_Q_W + ATT_OUT_W
ATT_BLOCK = 128
ROPE_THETA = 10000.0
RMS_EPS = 1e-6
LN_EPS = 1e-5

kernel_name = 'yoco_gated_deltanet_dilated_swa'


def layer_norm(x, g, b):
    xf = x.astype(jnp.float32)
    mu = jnp.mean(xf, -1, keepdims=True)
    var = jnp.mean(jnp.square(xf - mu), -1, keepdims=True)
    return ((xf - mu) * lax.rsqrt(var + LN_EPS) * g.astype(jnp.float32) + b.astype(jnp.float32)).astype(x.dtype)


def l2norm(x):
    return x * lax.rsqrt(jnp.sum(x * x, -1, keepdims=True) + RMS_EPS)


def rope_tables(seq_len):
    inv = 1.0 / (ROPE_THETA ** (jnp.arange(0, HEAD_DIM, 2, dtype=jnp.float32) / HEAD_DIM))
    ang = jnp.arange(seq_len, dtype=jnp.float32)[:, None] * inv[None, :]
    ang = jnp.concatenate([ang, ang], -1)
    return jnp.cos(ang), jnp.sin(ang)


def apply_rope(x, cos, sin):
    half = HEAD_DIM // 2
    rot = jnp.concatenate([-x[..., half:], x[..., :half]], -1)
    shape = (1, x.shape[1]) + (1,) * (x.ndim - 3) + (HEAD_DIM,)
    return x * cos.reshape(shape) + rot * sin.reshape(shape)


def causal_depthwise_conv(x, w):
    return lax.conv_general_dilated(
        x, w[:, None, :].astype(x.dtype), window_strides=(1,), padding=[(CONV_K - 1, 0)],
        dimension_numbers=('NWC', 'WIO', 'NWC'), feature_group_count=x.shape[-1])


def chunk_gated_delta_rule(q, k, v, g, beta):
    B, S, H, DK = q.shape
    DV = v.shape[-1]
    N = S // CHUNK

    def chunks(a):
        a = a.reshape((B, N, CHUNK, H) + a.shape[3:])
        return jnp.moveaxis(a, 3, 1)

    q, k, v, beta = chunks(q), chunks(k), chunks(v), chunks(beta)
    g = jnp.cumsum(chunks(g), -1)
    idx = jnp.arange(CHUNK)
    incl = idx[:, None] >= idx[None, :]
    strict = idx[:, None] > idx[None, :]
    decay = jnp.where(incl, jnp.exp(jnp.where(incl, g[..., :, None] - g[..., None, :], 0.0)), 0.0)
    kb = k * beta[..., None]
    a_mat = jnp.where(strict, jnp.einsum('bhnid,bhnjd->bhnij', kb, k) * decay, 0.0)
    eye = jnp.eye(CHUNK, dtype=jnp.float32)
    t_mat = lax.linalg.triangular_solve(eye + a_mat, jnp.broadcast_to(eye, a_mat.shape),
                                        left_side=True, lower=True, unit_diagonal=True)
    u = jnp.einsum('bhnij,bhnjv->bhniv', t_mat, v * beta[..., None])
    w = jnp.einsum('bhnij,bhnjk->bhnik', t_mat, kb * jnp.exp(g)[..., None])
    qk = jnp.einsum('bhnid,bhnjd->bhnij', q, k) * decay
    qg = q * jnp.exp(g)[..., None]
    g_last = g[..., -1]
    kg = k * jnp.exp(g_last[..., None] - g)[..., None]

    def step(state, inp):
        qk_n, u_n, w_n, qg_n, kg_n, gl_n = inp
        v_new = u_n - jnp.einsum('bhck,bhkv->bhcv', w_n, state)
        o = jnp.einsum('bhck,bhkv->bhcv', qg_n, state) + jnp.einsum('bhij,bhjv->bhiv', qk_n, v_new)
        state = state * jnp.exp(gl_n)[..., None, None] + jnp.einsum('bhck,bhcv->bhkv', kg_n, v_new)
        return state, o

    xs = (jnp.moveaxis(qk, 2, 0), jnp.moveaxis(u, 2, 0), jnp.moveaxis(w, 2, 0),
          jnp.moveaxis(qg, 2, 0), jnp.moveaxis(kg, 2, 0), jnp.moveaxis(g_last, 2, 0))
    state0 = jnp.zeros((B, H, DK, DV), jnp.float32)
    _, o = lax.scan(step, state0, xs)
    return jnp.transpose(o, (1, 0, 3, 2, 4)).reshape(B, S, H, DV)


def gated_deltanet_mixer(h, w_in, conv_w, a_log, dt_bias, norm_g, w_out):
    B, S, _ = h.shape
    f32 = jnp.float32
    proj = h @ w_in
    qkv = jax.nn.silu(causal_depthwise_conv(proj[..., :GDN_CONV_W], conv_w)).astype(f32)
    z = proj[..., GDN_CONV_W:GDN_CONV_W + GDN_V_W].astype(f32)
    b_raw = proj[..., GDN_CONV_W + GDN_V_W:GDN_CONV_W + GDN_V_W + GDN_HEADS].astype(f32)
    a_raw = proj[..., GDN_CONV_W + GDN_V_W + GDN_HEADS:].astype(f32)
    q = l2norm(qkv[..., :GDN_QK_W].reshape(B, S, GDN_HEADS, GDN_DK)) * (GDN_DK ** -0.5)
    k = l2norm(qkv[..., GDN_QK_W:2 * GDN_QK_W].reshape(B, S, GDN_HEADS, GDN_DK))
    v = qkv[..., 2 * GDN_QK_W:].reshape(B, S, GDN_HEADS, GDN_DV)
    beta = jax.nn.sigmoid(b_raw)
    g = -jnp.exp(a_log.astype(f32)) * jax.nn.softplus(a_raw + dt_bias.astype(f32))
    o = chunk_gated_delta_rule(q, k, v, g, beta)
    o = o * lax.rsqrt(jnp.mean(o * o, -1, keepdims=True) + RMS_EPS) * norm_g.astype(f32)
    o = o * jax.nn.silu(z.reshape(B, S, GDN_HEADS, GDN_DV))
    return o.reshape(B, S, GDN_V_W).astype(h.dtype) @ w_out


def shared_kv(h, kv_w, cos, sin):
    B, S, _ = h.shape
    kv = (h @ kv_w).astype(jnp.float32)
    k = kv[..., :ATT_Q_W].reshape(B, S, N_GROUPS, HEADS_PER_GROUP, HEAD_DIM)
    v = kv[..., ATT_Q_W:].reshape(B, S, N_GROUPS, HEADS_PER_GROUP, HEAD_DIM)
    return apply_rope(k, cos, sin), v


def dilated_window_attention(q, k, v, window, dilation):
    B, S, H, D = q.shape
    steps = window // dilation
    span = dilation * ATT_BLOCK
    s_pad = -(-S // span) * span
    pad = s_pad - S
    if pad:
        padw = ((0, 0), (0, pad), (0, 0), (0, 0))
        q, k, v = jnp.pad(q, padw), jnp.pad(k, padw), jnp.pad(v, padw)
    nb = s_pad // span

    def to_blocks(a):
        return a.reshape(B, nb, ATT_BLOCK, dilation, H, D).transpose(0, 3, 4, 1, 2, 5)

    def with_prev(a):
        prev = jnp.concatenate([jnp.zeros_like(a[:, :, :, :1]), a[:, :, :, :-1]], axis=3)
        return jnp.concatenate([prev, a], axis=-2)

    qb = to_blocks(q)
    kk = with_prev(to_blocks(k))
    vv = with_prev(to_blocks(v))
    i = jnp.arange(ATT_BLOCK)[:, None]
    j = jnp.arange(2 * ATT_BLOCK)[None, :]
    off = i + ATT_BLOCK - j
    band = (off >= 0) & (off <= steps)
    blk = jnp.arange(nb)[:, None, None]
    mask = band[None] & ((blk > 0) | (j >= ATT_BLOCK)[None])
    s = jnp.einsum('brhnqd,brhnkd->brhnqk', qb, kk)
    s = jnp.where(mask, s, -jnp.inf)
    m = jnp.max(s, -1, keepdims=True)
    p = jnp.exp(s - m)
    den = jnp.sum(p, -1, keepdims=True)
    o = jnp.einsum('brhnqk,brhnkd->brhnqd', p, vv) / den
    lse = (m + jnp.log(den))[..., 0]
    o = o.transpose(0, 3, 4, 1, 2, 5).reshape(B, s_pad, H, D)[:, :S]
    lse = lse.transpose(0, 3, 4, 1, 2).reshape(B, s_pad, H)[:, :S]
    return o, lse


def dilated_attention_mixer(h, k_sh, v_sh, w_in, w_out, cos, sin):
    B, S, _ = h.shape
    proj = h @ w_in
    q = proj[..., :ATT_Q_W].astype(jnp.float32).reshape(B, S, N_GROUPS, HEADS_PER_GROUP, HEAD_DIM)
    z = proj[..., ATT_Q_W:].astype(jnp.float32)
    q = apply_rope(q, cos, sin) * (HEAD_DIM ** -0.5)
    outs, lses = [], []
    for gi, (window, dilation) in enumerate(DIL_GROUPS):
        o, lse = dilated_window_attention(q[:, :, gi], k_sh[:, :, gi], v_sh[:, :, gi], window, dilation)
        outs.append(o)
        lses.append(lse)
    wts = jax.nn.softmax(jnp.stack(lses, 0), axis=0)
    o = jnp.sum(wts[..., None] * jnp.stack(outs, 0), 0).reshape(B, S, ATT_OUT_W)
    o = o * jax.nn.silu(z)
    return o.astype(h.dtype) @ w_out


def setup_inputs(seed: int = 0) -> dict:
    key = jax.random.key(seed)
    ks = jax.random.split(key, 14)
    n_a = DEPTH // 2
    n_b = DEPTH - n_a
    beta_init = (8.0 * DEPTH) ** -0.25
    f32 = jnp.float32
    x = jax.random.normal(ks[0], (BATCH, SEQ, D_MODEL), f32)
    ln_g = 1.0 + 0.02 * jax.random.normal(ks[1], (DEPTH, D_MODEL), f32)
    ln_b = 0.02 * jax.random.normal(ks[2], (DEPTH, D_MODEL), f32)
    gdn_w_in = jax.random.normal(ks[3], (n_a, D_MODEL, GDN_IN_W), f32) * D_MODEL ** -0.5
    gdn_conv_w = jax.random.normal(ks[4], (n_a, CONV_K, GDN_CONV_W), f32) * CONV_K ** -0.5
    gdn_a_log = jnp.log(jax.random.uniform(ks[5], (n_a, GDN_HEADS), f32, 1.0, 16.0))
    dt = jnp.exp(jax.random.uniform(ks[6], (n_a, GDN_HEADS), f32, math.log(1e-3), math.log(1e-1)))
    gdn_dt_bias = dt + jnp.log(-jnp.expm1(-dt))
    gdn_norm_g = 1.0 + 0.02 * jax.random.normal(ks[7], (n_a, GDN_DV), f32)
    gdn_w_out = jax.random.normal(ks[8], (n_a, GDN_V_W, D_MODEL), f32) * (GDN_V_W ** -0.5) * beta_init
    kv_w = jax.random.normal(ks[9], (D_MODEL, 2 * ATT_Q_W), f32) * D_MODEL ** -0.5
    att_w_in = jax.random.normal(ks[10], (n_b, D_MODEL, ATT_IN_W), f32) * D_MODEL ** -0.5
    att_w_out = jax.random.normal(ks[11], (n_b, ATT_OUT_W, D_MODEL), f32) * (ATT_OUT_W ** -0.5) * beta_init
    return {'x': x, 'ln_g': ln_g, 'ln_b': ln_b, 'gdn_w_in': gdn_w_in, 'gdn_conv_w': gdn_conv_w,
            'gdn_a_log': gdn_a_log, 'gdn_dt_bias': gdn_dt_bias, 'gdn_norm_g': gdn_norm_g,
            'gdn_w_out': gdn_w_out, 'kv_w': kv_w, 'att_w_in': att_w_in, 'att_w_out': att_w_out}


def reference(x, ln_g, ln_b, gdn_w_in, gdn_conv_w, gdn_a_log, gdn_dt_bias, gdn_norm_g,
              gdn_w_out, kv_w, att_w_in, att_w_out):
    n_a = DEPTH // 2
    alpha = (2.0 * DEPTH) ** 0.25
    cos, sin = rope_tables(x.shape[1])
    h = x
    k_sh = None
    v_sh = None
    for layer in range(DEPTH):
        if layer < n_a:
            y = gated_deltanet_mixer(h, gdn_w_in[layer], gdn_conv_w[layer], gdn_a_log[layer],
                                     gdn_dt_bias[layer], gdn_norm_g[layer], gdn_w_out[layer])
        else:
            if layer == n_a:
                k_sh, v_sh = shared_kv(h, kv_w, cos, sin)
            jb = layer - n_a
            y = dilated_attention_mixer(h, k_sh, v_sh, att_w_in[jb], att_w_out[jb], cos, sin)
        h = layer_norm(alpha * h + y, ln_g[layer], ln_b[layer])
    return h
```

```python
import contextlib
import math
import numpy as np
import concourse.bass as bass
import concourse.mybir as mybir
from concourse.bass_utils import run_bass_kernel_spmd

F32 = mybir.dt.float32
BF16 = mybir.dt.bfloat16
AF = mybir.ActivationFunctionType
ALU = mybir.AluOpType
AX = mybir.AxisListType
NEG = -30000.0
import os
BSTOP = int(os.environ.get('BSTOP', '99'))
BSUB = int(os.environ.get('BSUB', '99'))


class Dep:
    __slots__ = ("w", "r", "x")

    def __init__(self):
        self.w = None
        self.r = {}
        self.x = False


class Prog:
    ENGS = ("pe", "act", "dve", "pool", "sp")

    def __init__(self, nc, n_dma_sems=32, epoch=4000):
        self.nc = nc
        self.ops = []
        self.n_dma_sems = n_dma_sems
        self.epoch = epoch

    def add(self, eng, fn, reads=(), writes=(), dma=False):
        idx = len(self.ops)
        deps = set()
        if any(t.x for t in reads):
            writes = list(writes) + [t for t in reads if t.x and t not in writes]
            reads = [t for t in reads if not t.x]
        for t in reads:
            if t.w is not None:
                deps.add(t.w)
        for t in writes:
            if t.w is not None:
                deps.add(t.w)
            deps.update(t.r.values())
        rk = ("dma", idx) if dma else eng
        for t in reads:
            t.r[rk] = idx
        for t in writes:
            t.w = idx
            t.r = {}
        deps.discard(idx)
        self.ops.append((eng, fn, deps, dma))
        return idx

    def dma(self, eng, out, in_, reads=(), writes=()):
        return self.add(eng, lambda e: e.dma_start(out=out, in_=in_), reads, writes, dma=True)

    def barrier(self):
        last = {}
        dmas = set()
        for i, (eng, fn, deps, is_dma) in enumerate(self.ops):
            if is_dma:
                dmas.add(i)
            elif fn is not None:
                last[eng] = i
        deps = set(last.values()) | dmas
        for e in self.ENGS:
            self.ops.append((e, None, set(deps), False))

    def emit(self):
        nc = self.nc
        ops = self.ops
        n = len(ops)
        signal = [False] * n
        for i, (eng, fn, deps, is_dma) in enumerate(ops):
            for d in deps:
                deng, _, _, ddma = ops[d]
                if ddma:
                    continue
                if deng != eng or is_dma or eng != "pe":
                    signal[d] = True
        cnt = {e: 0 for e in self.ENGS}
        sig_of = [None] * n
        dma_of = [None] * n
        ndma = 0
        for i, (eng, fn, deps, is_dma) in enumerate(ops):
            if is_dma:
                dma_of[i] = (ndma % self.n_dma_sems, 16 * (ndma // self.n_dma_sems + 1))
                ndma += 1
            elif signal[i]:
                c = cnt[eng]
                sig_of[i] = (eng, c // self.epoch, c % self.epoch + 1)
                cnt[eng] = c + 1
        n_epochs = {e: (cnt[e] + self.epoch - 1) // self.epoch for e in self.ENGS}
        self.stats = dict(n_ops=n, n_dma=ndma, signals=dict(cnt))
        with contextlib.ExitStack() as st:
            esems = {}
            for e in self.ENGS:
                for k in range(n_epochs[e]):
                    esems[(e, k)] = st.enter_context(nc.semaphore(f"s_{e}_{k}"))
            dsems = [st.enter_context(nc.semaphore(f"s_dma_{k}")) for k in range(min(self.n_dma_sems, max(ndma, 1)))]
            block = st.enter_context(nc.Block())

            def make(engname):
                def body(e):
                    known = {}

                    def wait(sem, key, val):
                        if known.get(key, 0) < val:
                            e.wait_ge(sem, val)
                            known[key] = val

                    for i, (eng, fn, deps, is_dma) in enumerate(ops):
                        if eng != engname:
                            continue
                        need = {}
                        for d in deps:
                            if ops[d][3]:
                                k, v = dma_of[d]
                                key = ("d", k)
                                if need.get(key, (None, 0))[1] < v:
                                    need[key] = (dsems[k], v)
                            else:
                                if sig_of[d] is None:
                                    continue
                                if ops[d][0] == "pe" and eng == "pe" and not is_dma:
                                    continue
                                se, ep, v = sig_of[d]
                                key = (se, ep)
                                if need.get(key, (None, 0))[1] < v:
                                    need[key] = (esems[key], v)
                        if is_dma:
                            k, v = dma_of[i]
                            if v > 16:
                                key = ("d", k)
                                if need.get(key, (None, 0))[1] < v - 16:
                                    need[key] = (dsems[k], v - 16)
                        for key, (sem, v) in need.items():
                            wait(sem, key, v)
                        if fn is None:
                            continue
                        inst = fn(e)
                        if is_dma:
                            k, v = dma_of[i]
                            inst.then_inc(dsems[k], 16)
                        elif sig_of[i] is not None:
                            se, ep, v = sig_of[i]
                            inst.then_inc(esems[(se, ep)], 1)
                    if engname == "sp":
                        last = {}
                        for i in range(n):
                            if dma_of[i] is not None:
                                k, v = dma_of[i]
                                last[k] = max(last.get(k, 0), v)
                        for k, v in last.items():
                            wait(dsems[k], ("d", k), v)
                return body

            block.tensor(make("pe"))
            block.scalar(make("act"))
            block.vector(make("dve"))
            block.gpsimd(make("pool"))
            block.sync(make("sp"))


class B:
    def __init__(self, h, nd=1):
        self.h = h
        self.d = Dep()
        self.ds = [Dep() for _ in range(nd)] if nd > 1 else [self.d]
        self.held = False


class Ring:
    def __init__(self, items):
        self.items = items
        self.i = 0

    def next(self, hold=False):
        n = len(self.items)
        for _ in range(n):
            b = self.items[self.i % n]
            self.i += 1
            if not b.held:
                b.held = hold
                return b
        raise AssertionError("ring exhausted (all buffers held)")


def rel(*bs):
    for b in bs:
        b.held = False


def pipeline(tasks, width):
    tasks = iter(tasks)
    active = []
    done = False
    while True:
        if len(active) < width and not done:
            try:
                active.append(next(tasks))
            except StopIteration:
                done = True
        if not active:
            if done:
                break
            continue
        for g in list(active):
            try:
                next(g)
            except StopIteration:
                active.remove(g)


D_MODEL = 1024
GH = 8
CW_A = 6160
ALPHA = 4.0 ** 0.25


class K:
    def __init__(self, S, dbg=False):
        self.S = S
        self.NT = S // 128
        self.NG = S // 512
        self.dbg = dbg
        self.nc = nc = bass.Bass("TRN2", target_bir_lowering=False)
        self.P = Prog(nc)
        self.st = contextlib.ExitStack()
        self.cnt = 0
        self.sb_ptr = 16512
        NT, NG = self.NT, self.NG

        def din(name, shape, dt=F32):
            return nc.dram_tensor(name, shape, dt, kind="ExternalInput").ap()

        self.x = din("x", [S, 1024])
        self.w_in = din("w_in", [1024, CW_A])
        self.w_out = din("w_out", [2048, 1024])
        self.kv_w = din("kv_w", [1024, 3072])
        self.aw_in = din("aw_in", [1024, 2048])
        self.aw_out = din("aw_out", [512, 1024])
        self.cwd = din("cw", [128, 32 * 4])
        self.vecs = din("vecs", [1, 4 * 1024 + 256 + 16])
        self.consts = din("consts", [128, 1024])
        self.rope = din("rope", [4, 128, S])
        self.out = nc.dram_tensor("out", [S, 1024], F32, kind="ExternalOutput").ap()
        kind = "ExternalOutput" if dbg else "Internal"

        def scr(name, shape, dt):
            return nc.dram_tensor(name, shape, dt, kind=kind).ap()

        self.qT_S = scr("qT_S", [NG, 128, 8 * 512], BF16)
        self.kT_S = scr("kT_S", [NG, 128, 8 * 512], BF16)
        self.zT_S = scr("zT_S", [NG, 128, 16 * 512], BF16)
        self.ktok_S = scr("ktok_S", [S, 1024], BF16)
        self.vtok_S = scr("vtok_S", [S, 2048], BF16)
        self.bg_S = scr("bg_S", [128, NT * 16], F32)
        self.bgr_S = scr("bgr_S", [128, NT * 16], F32)
        self.Tt_S = scr("Tt_S", [NT, 128, 1024], BF16)
        self.DT_S = scr("DT_S", [NT, 128, 1024], BF16)
        self.gb_S = scr("gb_S", [NT, 128, 1024], BF16)
        self.sm_S = scr("sm_S", [NT, 128, 64], F32)
        self.h1_S = scr("h1_S", [S, 1024], F32)
        self.QT_S = scr("QT_S", [NG, 128, 12 * 512], BF16)
        self.KT_S = scr("KT_S", [NG, 128, 12 * 512], BF16)
        self.V_S = scr("V_S", [S, 1536], BF16)
        self.Z_S = scr("Z_S", [S, 512], BF16)
        self.O_S = scr("O_S", [3, S, 512], F32)
        self.L_S = scr("L_S", [3, S, 4], F32)
        self.sd = {}
        self.banks = Ring([B(self.st.enter_context(nc.psum_tensor(f"bank{i}", [128, 512], F32))) for i in range(8)])
        for b in self.banks.items:
            b.d.x = True

    def sdep(self, *key):
        if key not in self.sd:
            self.sd[key] = Dep()
        return self.sd[key]

    def sb(self, name, shape, dt=F32, nd=1):
        self.cnt += 1
        nb = int(np.prod(shape[1:])) * (2 if dt == BF16 else 4)
        nb = (nb + 31) // 32 * 32
        off = self.sb_ptr
        self.sb_ptr += nb
        assert self.sb_ptr <= 229344, (name, self.sb_ptr)
        return B(self.nc.alloc_sbuf_tensor_at(f"{name}_{self.cnt}", shape, dt, offset=off), nd)

    def new_phase(self):
        self.sb_ptr = self.sb_glob
        self.P.barrier()

    def ring(self, name, n, shape, dt=F32, nd=1):
        return Ring([self.sb(name, shape, dt, nd) for _ in range(n)])

    def bank(self, hold=False):
        return self.banks.next(hold)

    def copy(self, eng, out, in_, reads, writes):
        if eng == "act":
            self.P.add("act", lambda e: e.copy(out, in_), reads, writes)
        else:
            self.P.add(eng, lambda e: e.tensor_copy(out, in_), reads, writes)

    def tt(self, eng, out, a, b, op, reads, writes):
        self.P.add(eng, lambda e: e.tensor_tensor(out, a, b, op), reads, writes)

    def stt(self, eng, out, a, sc, b, op0, op1, reads, writes):
        self.P.add("dve", lambda e: e.scalar_tensor_tensor(out, a, sc, b, op0, op1), reads, writes)

    def actf(self, out, in_, func, reads, writes, **kw):
        self.P.add("act", lambda e: e.activation(out, in_, func, **kw), reads, writes)

    def mm(self, out, lhsT, rhs, start, stop, reads, writes):
        self.P.add("pe", lambda e: e.matmul(out, lhsT, rhs, start=start, stop=stop), reads, writes)

    def tr(self, out, in_, ident, reads, writes):
        self.P.add("pe", lambda e: e.transpose(out, in_, ident), reads, writes)

    def load_consts(self):
        P = self.P
        c = self.cst = self.sb("cst", [128, 1024])
        P.dma("sp", c.h[:], self.consts, writes=[c.d])
        self.identF = c.h[:, 0:128]
        self.Umat = c.h[:, 128:256]
        self.NEGU = c.h[:, 256:384]
        self.NEGL = c.h[:, 384:512]
        self.onesF = c.h[:, 512:640]
        self.maskA = c.h[:, 768:1024]
        cb = self.cstb = self.sb("cstb", [128, 384], BF16)
        self.copy("dve", cb.h[:, 0:128], c.h[:, 0:128], [c.d], [cb.d])
        self.copy("dve", cb.h[:, 128:256], c.h[:, 640:768], [c.d], [cb.d])
        self.copy("dve", cb.h[:, 256:384], c.h[:, 512:640], [c.d], [cb.d])
        self.onesB = cb.h[:, 256:384]
        self.identB = cb.h[:, 0:128]
        self.permB = cb.h[:, 128:256]

    def load_weight(self, dst, ddeps, src, ncols, stg, blk=512, engs=("dve", "act", "pool")):
        self._kk = src.shape[0] // 128
        nb = (ncols + blk - 1) // blk
        for cbk in range(nb):
            c0 = cbk * blk
            cw = min(blk, ncols - c0)
            s = stg.next()
            kk = self._kk
            self.P.dma("sp", s.h[:, 0:kk, 0:cw], src[:, c0:c0 + cw].rearrange("(k p) n -> p k n", p=128), writes=[s.d])
            self.copy(engs[cbk % len(engs)], dst.h[:, :, c0:c0 + cw], s.h[:, 0:kk, 0:cw], [s.d], [ddeps[cbk]])

    def load_xT(self, src, g, xfr, xT):
        P = self.P
        xfs = []
        for t in range(4):
            xf = xfr.next()
            T = g * 4 + t
            P.dma("sp", xf.h[:], src[T * 128:(T + 1) * 128, :], reads=[self.sdep(id(src), T)] if src is self.h1_S else [], writes=[xf.d])
            for half in range(2):
                pb = self.bank()
                for kk in range(4):
                    k = half * 4 + kk
                    self.tr(pb.h[:, kk * 128:(kk + 1) * 128], xf.h[:, k * 128:(k + 1) * 128], self.identF, [xf.d, self.cst.d], [pb.d])
                self.copy("act" if half else "dve", xT.h[:, half * 4:half * 4 + 4, t * 128:(t + 1) * 128],
                          pb.h[:].rearrange("p (k n) -> p k n", k=4), [pb.d], [xT.ds[t]])
            xfs.append(xf)
        return xfs

    def resid_ln(self, layer, xf, ybanks, dst_ap, dst_dep, hp, small):
        for _ in self.resid_ln_g(layer, xf, ybanks, dst_ap, dst_dep, hp, small):
            pass

    def resid_ln_g(self, layer, xf, ybanks, dst_ap, dst_dep, hp, small):
        P = self.P
        for half in range(2):
            bk = ybanks[half]
            self.stt("dve", hp.h[:, half * 512:(half + 1) * 512], xf.h[:, half * 512:(half + 1) * 512], ALPHA, bk.h[:],
                     ALU.mult, ALU.add, [xf.d, bk.d], [hp.d])
        rel(*ybanks)
        stt_ = small.h
        P.add("dve", lambda e: e.bn_stats(stt_[:, 0:6], hp.h[:, 0:512]), [hp.d], [small.d])
        P.add("dve", lambda e: e.bn_stats(stt_[:, 6:12], hp.h[:, 512:1024]), [hp.d], [small.d])
        P.add("dve", lambda e: e.bn_aggr(stt_[:, 12:14], stt_[:, 0:12].rearrange("p (a b) -> p a b", a=2)), [small.d], [small.d])
        yield
        self.actf(stt_[:, 14:15], stt_[:, 13:14], AF.Sqrt, [small.d], [small.d], bias=self.eps5.h[:, 0:1], scale=1.0)
        yield
        P.add("dve", lambda e: e.reciprocal(stt_[:, 15:16], stt_[:, 14:15]), [small.d], [small.d])
        P.add("dve", lambda e: e.tensor_scalar(hp.h[:], hp.h[:], stt_[:, 12:13], stt_[:, 15:16], ALU.subtract, ALU.mult),
              [hp.d, small.d], [hp.d])
        yield
        self.tt("pool", hp.h[:], hp.h[:], self.lnv.h[:, 0:1024], ALU.mult, [hp.d, self.lnv.d], [hp.d])
        self.tt("pool", hp.h[:], hp.h[:], self.lnv.h[:, 1024:2048], ALU.add, [hp.d, self.lnv.d], [hp.d])
        P.dma("pool", dst_ap, hp.h[:], reads=[hp.d], writes=[dst_dep])

    def phaseA(self):
        P, NG = self.P, self.NG
        wA = self.sb("wA", [128, 8, CW_A], BF16)
        stg = self.ring("wstgA", 2, [128, 8, 128])
        wdeps = [Dep() for _ in range(49)]
        self.load_weight(wA, wdeps, self.w_in, CW_A, stg, blk=128)
        cw = self.sb("cw", [128, 32, 4])
        P.dma("sp", cw.h[:], self.cwd.rearrange("p (c j) -> p c j", j=4), writes=[cw.d])
        nhalf = self.sb("nhalf", [128, 512])
        P.add("pool", lambda e: e.memset(nhalf.h[:], -0.5), [], [nhalf.d])
        halo = self.sb("halo", [128, 32, 4], BF16, nd=32)
        P.add("pool", lambda e: e.memset(halo.h[:], 0.0), [], halo.ds)
        W = 6
        xfr = self.ring("xfA", 2, [128, 1024])
        xTs = [self.sb("xTA", [128, 8, 512], BF16, nd=4) for _ in range(2)]
        prer = self.ring("pre", W + 1, [128, 516], BF16)
        dgr = self.ring("dg", 3, [128, 4, 128], BF16)
        slr = self.ring("sl", 3, [128, 512])
        sqr = self.ring("sq", 3, [128, 512], BF16)
        rnr = self.ring("rn", 3, [128, 512])
        ocr = self.ring("oc", W + 1, [128, 512], BF16)
        ktokG = self.sb("ktokG", [128, 4, 1024], BF16)
        vtokG = self.sb("vtokG", [128, 4, 2048], BF16)
        bgr = self.ring("bgt", 2, [128, 16])

        def xload(g, xT):
            for t in range(4):
                T = g * 4 + t
                xf = xfr.next()
                P.dma("sp", xf.h[:], self.x[T * 128:(T + 1) * 128, :], writes=[xf.d])
                for half in range(2):
                    pb = self.bank()
                    for kk in range(4):
                        k = half * 4 + kk
                        self.tr(pb.h[:, kk * 128:(kk + 1) * 128], xf.h[:, k * 128:(k + 1) * 128], self.identF, [xf.d, self.cst.d], [pb.d])
                    self.copy("act" if half else "dve", xT.h[:, half * 4:half * 4 + 4, t * 128:(t + 1) * 128],
                              pb.h[:].rearrange("p (k n) -> p k n", k=4), [pb.d], [xT.ds[t]])
                yield
                pb = self.bank()
                for k in range(8):
                    self.mm(pb.h[:, 0:16], xT.h[:, k, t * 128:(t + 1) * 128], wA.h[:, k, 6144:6160], k == 0, k == 7,
                            [xT.ds[t], wdeps[48]], [pb.d])
                bgt = bgr.next()
                self.copy("dve", bgt.h[:], pb.h[:, 0:16], [pb.d], [bgt.d])
                P.dma("pool", self.bgr_S[:, T * 16:(T + 1) * 16], bgt.h[:], reads=[bgt.d], writes=[Dep()])
                yield

        def chunk(g, c, xT):
            pb = self.bank(hold=True)
            for k in range(8):
                self.mm(pb.h[:], wA.h[:, k, c * 128:(c + 1) * 128], xT.h[:, k, :], k == 0, k == 7, [wdeps[c]] + xT.ds, [pb.d])
            yield
            oc = ocr.next(hold=True)
            if c >= 32:
                self.actf(oc.h[:], pb.h[:], AF.Silu, [pb.d], [oc.d])
                rel(pb)
                P.dma("pool", self.zT_S[g, :, (c - 32) * 512:(c - 31) * 512], oc.h[:], reads=[oc.d], writes=[self.sdep("zT", g)])
                rel(oc)
                return
            pre, dg = prer.next(hold=True), dgr.next(hold=True)
            self.copy("pool", pre.h[:, 0:3], halo.h[:, c, 0:3], [halo.ds[c]], [pre.d])
            self.copy("act", pre.h[:, 3:515], pb.h[:], [pb.d], [pre.d])
            rel(pb)
            self.copy("pool", halo.h[:, c, 0:3], pre.h[:, 512:515], [pre.d], [halo.ds[c]])
            self.tt("pool", dg.h[:], self.identF.unsqueeze(1).to_broadcast([128, 4, 128]),
                    cw.h[:, c, :].unsqueeze(2).to_broadcast([128, 4, 128]), ALU.mult, [self.cst.d, cw.d], [dg.d])
            yield
            pb2 = self.bank(hold=True)
            for j in range(4):
                self.mm(pb2.h[:], dg.h[:, j, :], pre.h[:, j:j + 512], j == 0, j == 3, [dg.d, pre.d], [pb2.d])
            rel(pre, dg)
            yield
            if c >= 16:
                self.actf(oc.h[:], pb2.h[:], AF.Silu, [pb2.d], [oc.d])
                rel(pb2)
                yield
                tb = self.bank(hold=True)
                tbb = tb.h[:].bitcast(BF16)
                for t in range(4):
                    self.tr(tbb[:, t * 128:(t + 1) * 128], oc.h[:, t * 128:(t + 1) * 128], self.identB, [oc.d, self.cstb.d], [tb.d])
                yield
                cc = c - 16
                self.copy("dve", vtokG.h[:, :, cc * 128:(cc + 1) * 128], tbb[:, 0:512].rearrange("p (t n) -> p t n", t=4), [tb.d], [vtokG.d])
                rel(tb, oc)
                return
            sl, sq = slr.next(hold=True), sqr.next(hold=True)
            self.actf(sl.h[:], pb2.h[:], AF.Silu, [pb2.d], [sl.d])
            rel(pb2)
            self.actf(sq.h[:], sl.h[:], AF.Square, [sl.d], [sq.d])
            yield
            pn = self.bank(hold=True)
            self.mm(pn.h[:], self.onesB, sq.h[:], True, True, [self.cstb.d, sq.d], [pn.d])
            rel(sq)
            yield
            rn = rnr.next(hold=True)
            if c < 8:
                self.actf(rn.h[:], pn.h[:], AF.Identity, [pn.d], [rn.d], bias=self.eps6q.h[:, 0:1], scale=128.0)
            else:
                self.actf(rn.h[:], pn.h[:], AF.Identity, [pn.d], [rn.d], bias=self.eps6.h[:, 0:1], scale=1.0)
            rel(pn)
            yield
            self.tt("pool", rn.h[:], rn.h[:], nhalf.h[:], ALU.pow, [rn.d, nhalf.d], [rn.d])
            yield
            self.tt("dve", oc.h[:], sl.h[:], rn.h[:], ALU.mult, [sl.d, rn.d], [oc.d])
            rel(sl, rn)
            if c < 8:
                P.dma("pool", self.qT_S[g, :, c * 512:(c + 1) * 512], oc.h[:], reads=[oc.d], writes=[self.sdep("qT", g)])
                rel(oc)
                return
            h = c - 8
            P.dma("pool", self.kT_S[g, :, h * 512:(h + 1) * 512], oc.h[:], reads=[oc.d], writes=[self.sdep("kT", g)])
            yield
            tb = self.bank(hold=True)
            tbb = tb.h[:].bitcast(BF16)
            for t in range(4):
                self.tr(tbb[:, t * 128:(t + 1) * 128], oc.h[:, t * 128:(t + 1) * 128], self.identB, [oc.d, self.cstb.d], [tb.d])
            yield
            self.copy("act", ktokG.h[:, :, h * 128:(h + 1) * 128], tbb[:, 0:512].rearrange("p (t n) -> p t n", t=4), [tb.d], [ktokG.d])
            rel(tb, oc)

        def finalize(g):
            rows = slice(g * 512, (g + 1) * 512)
            P.dma("pool", self.ktok_S[rows, :].rearrange("(t p) n -> p t n", p=128), ktokG.h[:], reads=[ktokG.d],
                  writes=[self.sdep("ktok", g * 4 + t) for t in range(4)])
            P.dma("pool", self.vtok_S[rows, :].rearrange("(t p) n -> p t n", p=128), vtokG.h[:], reads=[vtokG.d],
                  writes=[self.sdep("vtok", g * 4 + t) for t in range(4)])
            yield

        order = []
        for i in range(8):
            order += [8 + i, 16 + 2 * i, 16 + 2 * i + 1]
        for i in range(8):
            order += [i, 32 + 2 * i, 32 + 2 * i + 1]

        def tasks():
            for g in range(NG):
                xT = xTs[g % 2]
                for i, c in enumerate(order):
                    yield chunk(g, c, xT)
                    if i == 0 and g + 1 < NG:
                        yield xload(g + 1, xTs[(g + 1) % 2])
                yield finalize(g)

        for _ in xload(0, xTs[0]):
            pass
        pipeline(tasks(), W)

    def phaseG(self):
        P, NT = self.P, self.NT
        raw = self.sb("bgraw", [128, NT, 16])
        out = self.sb("bgout", [128, NT, 16])
        t1 = self.sb("gt1", [128, NT, 8])
        t2 = self.sb("gt2", [128, NT, 8])
        t3 = self.sb("gt3", [128, NT, 8])
        vb = self.sb("vecG", [128, 16])
        negA = self.sb("negA", [128, 8])
        P.dma("sp", raw.h[:], self.bgr_S.rearrange("p (t c) -> p t c", c=16), writes=[raw.d])
        P.dma("sp", vb.h[:], self.vecs[:, 4352:4368].partition_broadcast(128), writes=[vb.d])
        alog, dtb = vb.h[:, 0:8], vb.h[:, 8:16]
        self.tt("dve", t1.h[:], raw.h[:, :, 8:16], dtb.unsqueeze(1).to_broadcast([128, NT, 8]), ALU.add, [raw.d, vb.d], [t1.d])
        P.add("dve", lambda e: e.tensor_scalar(t2.h[:], t1.h[:], -1.0, None, ALU.mult), [t1.d], [t2.d])
        self.tt("dve", t2.h[:], t2.h[:], t1.h[:], ALU.max, [t1.d, t2.d], [t2.d])
        self.actf(out.h[:, :, 0:8], raw.h[:, :, 0:8], AF.Sigmoid, [raw.d], [out.d])
        self.actf(negA.h[:], alog, AF.Exp, [vb.d], [negA.d])
        self.actf(t3.h[:], t2.h[:], AF.Exp, [t2.d], [t3.d], scale=-1.0)
        self.actf(t3.h[:], t3.h[:], AF.Ln, [t3.d], [t3.d], bias=self.one.h[:, 0:1], scale=1.0)
        P.add("dve", lambda e: e.tensor_scalar(negA.h[:], negA.h[:], -1.0, None, ALU.mult), [negA.d], [negA.d])
        self.stt("dve", t2.h[:], t1.h[:], 0.0, t3.h[:], ALU.max, ALU.add, [t1.d, t3.d, t2.d], [t2.d])
        self.tt("dve", out.h[:, :, 8:16], t2.h[:], negA.h[:].unsqueeze(1).to_broadcast([128, NT, 8]), ALU.mult, [t2.d, negA.d, out.d], [out.d])
        P.dma("pool", self.bg_S.rearrange("p (t c) -> p t c", c=16), out.h[:], reads=[out.d], writes=[Dep()])

    def phaseN(self):
        P, NT = self.P, self.NT
        WN = 3
        kTr = self.ring("kTgN", 2, [128, 8, 512], BF16)
        bgr = self.ring("bgN", WN, [128, 16])
        smr = self.ring("smN", WN, [128, 64])
        f32r = [self.ring(f"nf{i}", WN, [128, 8, 128]) for i in range(10)]
        b16r = [self.ring(f"nb{i}", WN, [128, 8, 128], BF16) for i in range(3)]
        kgroups = {}

        def getk(g):
            if g not in kgroups:
                bq = kTr.next()
                P.dma("sp", bq.h[:], self.kT_S[g].rearrange("p (h n) -> p h n", h=8), reads=[self.sdep("kT", g)], writes=[bq.d])
                kgroups[g] = bq
            return kgroups[g]

        def v4(bk):
            return bk.h[:].rearrange("p (h i) -> p h i", h=4)

        def banks2():
            return [self.bank(hold=True), self.bank(hold=True)]

        def ntile(T):
            g, t = divmod(T, 4)
            cs = slice(t * 128, (t + 1) * 128)
            kTg = getk(g)
            bufs = [r_.next(hold=True) for r_ in [bgr, smr] + f32r + b16r]
            bg, sm, R, ttb, tmp2, Dm, pcA, pcB, ptA, ptB, ttA, ttB_, TtB, DT, gbc = bufs
            tmp1, nbD = R, tmp2
            rows = slice(T * 128, (T + 1) * 128)
            P.dma("sp", bg.h[:], self.bg_S[:, T * 16:(T + 1) * 16], reads=[self.sdep("bg", T)], writes=[bg.d])
            beta = bg.h[:, 0:8]
            graw = bg.h[:, 8:16]
            b0 = self.bank(hold=True)
            self.mm(b0.h[:, 0:8], self.Umat, graw, True, True, [self.cst.d, bg.d], [b0.d])
            self.tt("dve", R.h[:], self.Umat.unsqueeze(1).to_broadcast([128, 8, 128]),
                    graw.unsqueeze(2).to_broadcast([128, 8, 128]), ALU.mult, [self.cst.d, bg.d], [R.d])
            yield
            self.copy("dve", sm.h[:, 0:8], b0.h[:, 0:8], [b0.d], [sm.d])
            rel(b0)
            bAB = banks2()
            self.mm(bAB[0].h[:], self.onesF, R.h[:, 0:4, :].rearrange("p h i -> p (h i)"), True, True, [self.cst.d, R.d], [bAB[0].d])
            self.mm(bAB[1].h[:], self.onesF, R.h[:, 4:8, :].rearrange("p h i -> p (h i)"), True, True, [self.cst.d, R.d], [bAB[1].d])
            yield
            for hf, bk in enumerate(bAB):
                hs = slice(hf * 4, hf * 4 + 4)
                self.tt("dve", ttb.h[:, hs, :], v4(bk), sm.h[:, hf * 4:hf * 4 + 4].unsqueeze(2).to_broadcast([128, 4, 128]),
                        ALU.subtract, [bk.d, sm.d], [ttb.d])
                self.actf(gbc.h[:, hs, :], v4(bk), AF.Exp, [bk.d], [gbc.d])
                self.actf(sm.h[:, 48 + hf * 4:52 + hf * 4], v4(bk)[:, :, 127], AF.Exp, [bk.d, sm.d], [sm.d])
            rel(*bAB)
            yield
            self.tt("dve", tmp1.h[:], ttb.h[:], self.NEGU.unsqueeze(1).to_broadcast([128, 8, 128]), ALU.add,
                    [ttb.d, self.cst.d], [tmp1.d])
            self.stt("dve", tmp2.h[:], ttb.h[:], -1.0, self.NEGL.unsqueeze(1).to_broadcast([128, 8, 128]), ALU.mult, ALU.add,
                     [ttb.d, self.cst.d], [tmp2.d])
            yield
            self.actf(DT.h[:], tmp1.h[:], AF.Exp, [tmp1.d], [DT.d])
            self.actf(Dm.h[:], tmp2.h[:], AF.Exp, [tmp2.d], [Dm.d])
            self.actf(sm.h[:, 8:16], sm.h[:, 0:8], AF.Exp, [sm.d], [sm.d])
            yield
            self.stt("dve", sm.h[:, 16:24], beta, -1.0, sm.h[:, 8:16], ALU.mult, ALU.mult, [bg.d, sm.d], [sm.d])
            P.add("dve", lambda e: e.tensor_scalar(sm.h[:, 24:32], beta, -1.0, None, ALU.mult), [bg.d, sm.d], [sm.d])
            self.tt("dve", nbD.h[:], Dm.h[:], sm.h[:, 24:32].unsqueeze(2).to_broadcast([128, 8, 128]), ALU.mult, [Dm.d, sm.d], [nbD.d])
            gb = banks2()
            for h in range(8):
                bk = gb[h // 4]
                self.mm(bk.h[:, (h % 4) * 128:(h % 4 + 1) * 128], kTg.h[:, h, cs], kTg.h[:, h, cs], True, True, [kTg.d], [bk.d])
            yield
            pc, pt, Tt = pcA, ptA, ttA
            for hf in range(2):
                hs = slice(hf * 4, hf * 4 + 4)
                self.tt("dve", pc.h[:, hs, :], v4(gb[hf]), nbD.h[:, hs, :], ALU.mult, [gb[hf].d, nbD.d], [pc.d])
            rel(*gb)
            yield
            tb = banks2()
            for h in range(8):
                bk = tb[h // 4]
                self.tr(bk.h[:, (h % 4) * 128:(h % 4 + 1) * 128], pc.h[:, h, :], self.identF, [pc.d, self.cst.d], [bk.d])
            yield
            for hf in range(2):
                hs = slice(hf * 4, hf * 4 + 4)
                self.copy("act", pt.h[:, hs, :], v4(tb[hf]), [tb[hf].d], [pt.d])
                self.tt("dve", Tt.h[:, hs, :], v4(tb[hf]), self.identF.unsqueeze(1).to_broadcast([128, 4, 128]), ALU.add,
                        [tb[hf].d, self.cst.d], [Tt.d])
            rel(*tb)
            yield
            for lvl in range(1, 7):
                pc2 = pcB if pc is pcA else pcA
                pt2 = ptB if pt is ptA else ptA
                Tt2 = ttB_ if Tt is ttA else ttA
                xb = banks2()
                for h in range(8):
                    bk = xb[h // 4]
                    self.mm(bk.h[:, (h % 4) * 128:(h % 4 + 1) * 128], pt.h[:, h, :], pc.h[:, h, :], True, True, [pt.d, pc.d], [bk.d])
                if lvl < 6:
                    yb = banks2()
                    for h in range(8):
                        bk = yb[h // 4]
                        self.mm(bk.h[:, (h % 4) * 128:(h % 4 + 1) * 128], pc.h[:, h, :], pt.h[:, h, :], True, True, [pt.d, pc.d], [bk.d])
                yield
                for hf in range(2):
                    hs = slice(hf * 4, hf * 4 + 4)
                    self.copy("act", pc2.h[:, hs, :], v4(xb[hf]), [xb[hf].d], [pc2.d])
                rel(*xb)
                if lvl < 6:
                    for hf in range(2):
                        hs = slice(hf * 4, hf * 4 + 4)
                        self.copy("dve" if hf else "act", pt2.h[:, hs, :], v4(yb[hf]), [yb[hf].d], [pt2.d])
                    rel(*yb)
                yield
                zb = banks2()
                for h in range(8):
                    bk = zb[h // 4]
                    self.mm(bk.h[:, (h % 4) * 128:(h % 4 + 1) * 128], pc2.h[:, h, :], Tt.h[:, h, :], True, True, [pc2.d, Tt.d], [bk.d])
                yield
                for hf in range(2):
                    hs = slice(hf * 4, hf * 4 + 4)
                    dst = Tt2 if lvl < 6 else TtB
                    self.tt("dve", dst.h[:, hs, :], v4(zb[hf]), Tt.h[:, hs, :], ALU.add, [zb[hf].d, Tt.d], [dst.d])
                rel(*zb)
                pc, pt, Tt = pc2, pt2, Tt2
                yield
            P.dma("pool", self.Tt_S[T].rearrange("p (h n) -> p h n", h=8), TtB.h[:], reads=[TtB.d], writes=[self.sdep("Tt", T)])
            P.dma("pool", self.DT_S[T].rearrange("p (h n) -> p h n", h=8), DT.h[:], reads=[DT.d], writes=[self.sdep("DT", T)])
            P.dma("pool", self.gb_S[T].rearrange("p (h n) -> p h n", h=8), gbc.h[:], reads=[gbc.d], writes=[self.sdep("gb", T)])
            P.dma("pool", self.sm_S[T], sm.h[:], reads=[sm.d], writes=[self.sdep("sm", T)])
            rel(*bufs)

        pipeline((ntile(T) for T in range(NT)), WN)

    def phaseB(self):
        P, NG, NT = self.P, self.NG, self.NT
        wo = self.sb("woB", [128, 16, 1024], BF16)
        stg = self.ring("wstgB", 2, [128, 16, 128])
        wod = [Dep() for _ in range(8)]
        self.load_weight(wo, wod, self.w_out, 1024, stg, blk=128)
        lnv = self.lnv = self.sb("lnvB", [128, 2048])
        P.dma("sp", lnv.h[:], self.vecs[:, 0:2048].partition_broadcast(128), writes=[lnv.d])
        ngv = self.vb = self.sb("ngvB", [128, 256])
        P.dma("sp", ngv.h[:], self.vecs[:, 4096:4352].partition_broadcast(128), writes=[ngv.d])
        self.normg = ngv.h[:]
        Sf = [self.sb(f"Sf{h}", [128, 256]) for h in range(8)]
        Sb_ = [self.sb(f"Sb{h}", [128, 256], BF16) for h in range(8)]
        for h in range(8):
            P.add("pool", lambda e, h=h: e.memset(Sf[h].h[:], 0.0), [], [Sf[h].d])
            P.add("pool", lambda e, h=h: e.memset(Sb_[h].h[:], 0.0), [], [Sb_[h].d])
        qTr = self.ring("qTgB", 2, [128, 8, 512], BF16)
        kTr = self.ring("kTgB", 2, [128, 8, 512], BF16)
        ktr = self.ring("ktokB", 2, [128, 8, 128], BF16)
        vtr = self.ring("vtokB", 2, [128, 8, 256], BF16)
        bgr = self.ring("bgB", 2, [128, 16])
        DTr = self.ring("DTB", 2, [128, 8, 128], BF16)
        gbr = self.ring("gbB", 2, [128, 8, 128], BF16)
        smr = self.ring("smB", 3, [128, 64])
        TtBr = self.ring("TtB", 2, [128, 8, 128], BF16)
        qgr = self.ring("qg", 2, [128, 8, 128], BF16)
        PTbr = self.ring("PTb", 2, [128, 8, 128], BF16)
        bvr = self.ring("bv", 2, [128, 8, 256], BF16)
        kgbr = self.ring("kgb", 2, [128, 8, 128], BF16)
        osbr = self.ring("osb", 2, [128, 8, 256], BF16)
        zTr = self.ring("zTgB", 2, [128, 16, 128], BF16)
        xfr = self.ring("xfB", 2, [128, 1024])
        onbr = self.ring("onb", 2, [128, 16, 128], BF16)
        ogTr = self.ring("ogT", 2, [128, 16, 128], BF16)
        hpr = self.ring("hpB", 2, [128, 1024])
        s2r = self.ring("s2B", 2, [128, 16])
        rbr = self.ring("rb", 4, [128, 256], BF16)
        vnr = self.ring("vnb", 4, [128, 256], BF16)
        junk = self.sb("junk", [128, 256])
        groups = {}

        def getgroup(g):
            if g not in groups:
                qTg, kTg = qTr.next(), kTr.next()
                P.dma("sp", qTg.h[:], self.qT_S[g].rearrange("p (h n) -> p h n", h=8), reads=[self.sdep("qT", g)], writes=[qTg.d])
                P.dma("sp", kTg.h[:], self.kT_S[g].rearrange("p (h n) -> p h n", h=8), reads=[self.sdep("kT", g)], writes=[kTg.d])
                groups[g] = (qTg, kTg)
            return groups[g]

        def v4(bk):
            return bk.h[:].rearrange("p (h i) -> p h i", h=4)

        recur_done = [False] * NT

        def pipeline_gen(tasks, width):
            tasks = iter(tasks)
            active = []
            done = False
            while True:
                if len(active) < width and not done:
                    try:
                        active.append(next(tasks))
                    except StopIteration:
                        done = True
                if not active:
                    if done:
                        return
                    continue
                for gq in list(active):
                    try:
                        next(gq)
                    except StopIteration:
                        active.remove(gq)
                yield "s"

        def tile_gen(T):
            g, t = divmod(T, 4)
            cs = slice(t * 128, (t + 1) * 128)
            rows = slice(T * 128, (T + 1) * 128)
            qTg, kTg = getgroup(g)
            pre = [r_.next(hold=True) for r_ in (ktr, vtr, bgr, DTr, gbr)]
            kt, vt, bg, DT, gbc = pre
            mid = [r_.next(hold=True) for r_ in (TtBr, qgr, PTbr, bvr, kgbr)]
            TtB, qg, PTb, bv, kgb = mid
            sm = smr.next(hold=True)
            ssd = Dep()
            P.dma("sp", kt.h[:], self.ktok_S[rows, :].rearrange("p (h n) -> p h n", h=8), reads=[self.sdep("ktok", T)], writes=[kt.d])
            P.dma("sp", vt.h[:], self.vtok_S[rows, :].rearrange("p (h n) -> p h n", h=8), reads=[self.sdep("vtok", T)], writes=[vt.d])
            P.dma("sp", bg.h[:], self.bg_S[:, T * 16:(T + 1) * 16], reads=[self.sdep("bg", T)], writes=[bg.d])
            P.dma("sp", sm.h[:], self.sm_S[T], reads=[self.sdep("sm", T)], writes=[sm.d])
            P.dma("sp", TtB.h[:], self.Tt_S[T].rearrange("p (h n) -> p h n", h=8), reads=[self.sdep("Tt", T)], writes=[TtB.d])
            P.dma("sp", DT.h[:], self.DT_S[T].rearrange("p (h n) -> p h n", h=8), reads=[self.sdep("DT", T)], writes=[DT.d])
            P.dma("sp", gbc.h[:], self.gb_S[T].rearrange("p (h n) -> p h n", h=8), reads=[self.sdep("gb", T)], writes=[gbc.d])
            beta = bg.h[:, 0:8]
            yield "s"
            self.tt("dve", qg.h[:], qTg.h[:, :, cs], gbc.h[:], ALU.mult, [qTg.d, gbc.d], [qg.d])
            pb2 = [self.bank(hold=True), self.bank(hold=True)]
            for h in range(8):
                bk = pb2[h // 4]
                self.mm(bk.h[:, (h % 4) * 128:(h % 4 + 1) * 128], kTg.h[:, h, cs], qTg.h[:, h, cs], True, True, [kTg.d, qTg.d], [bk.d])
            yield "s"
            for hf in range(2):
                hs = slice(hf * 4, hf * 4 + 4)
                self.tt("dve", PTb.h[:, hs, :], v4(pb2[hf]), DT.h[:, hs, :], ALU.mult, [pb2[hf].d, DT.d], [PTb.d])
            rel(*pb2)
            yield "s"
            self.tt("dve", bv.h[:], vt.h[:], beta.unsqueeze(2).to_broadcast([128, 8, 256]), ALU.mult, [vt.d, bg.d], [bv.d])
            self.tt("dve", kgb.h[:], kt.h[:], DT.h[:, :, 127:128].to_broadcast([128, 8, 128]), ALU.mult, [kt.d, DT.d], [kgb.d])
            P.add("pool", lambda e: e.memset(sm.h[:, 32:40], 0.0), [sm.d], [ssd])
            rel(*pre)
            while T > 0 and not recur_done[T - 1]:
                yield "s"
            yield "recur"
            osb = osbr.next(hold=True)

            def head(h):
                b1 = self.bank(hold=True)
                self.mm(b1.h[:, 0:256], kTg.h[:, h, cs], Sb_[h].h[:], True, True, [kTg.d, Sb_[h].d], [b1.d])
                yield
                rb = rbr.next(hold=True)
                self.stt("dve", rb.h[:], b1.h[:, 0:256], sm.h[:, 16 + h:17 + h], bv.h[:, h, :], ALU.mult, ALU.add,
                         [b1.d, sm.d, bv.d], [rb.d])
                rel(b1)
                yield
                b2 = self.bank(hold=True)
                self.mm(b2.h[:, 0:256], TtB.h[:, h, :], rb.h[:], True, True, [TtB.d, rb.d], [b2.d])
                rel(rb)
                yield
                vnb = vnr.next(hold=True)
                self.copy("act", vnb.h[:], b2.h[:, 0:256], [b2.d], [vnb.d])
                rel(b2)
                yield
                b3, b4 = self.bank(hold=True), self.bank(hold=True)
                self.mm(b3.h[:, 0:256], qg.h[:, h, :], Sb_[h].h[:], True, False, [qg.d, Sb_[h].d], [b3.d])
                self.mm(b3.h[:, 0:256], PTb.h[:, h, :], vnb.h[:], False, True, [PTb.d, vnb.d], [b3.d])
                self.mm(b4.h[:, 0:256], kgb.h[:, h, :], vnb.h[:], True, True, [kgb.d, vnb.d], [b4.d])
                rel(vnb)
                yield
                self.stt("dve", Sf[h].h[:], Sf[h].h[:], sm.h[:, 48 + h:49 + h], b4.h[:, 0:256], ALU.mult, ALU.add,
                         [Sf[h].d, sm.d, b4.d], [Sf[h].d])
                rel(b4)
                self.actf(junk.h[:], b3.h[:, 0:256], AF.Square, [b3.d], [junk.d, ssd], accum_out=sm.h[:, 32 + h:33 + h])
                self.copy("act", osb.h[:, h, :], b3.h[:, 0:256], [b3.d], [osb.d])
                rel(b3)
                yield
                self.copy("act", Sb_[h].h[:], Sf[h].h[:], [Sf[h].d], [Sb_[h].d])

            yield from pipeline_gen((head(h) for h in range(8)), 3)
            recur_done[T] = True
            rel(*mid)
            yield "post"
            post = [r_.next(hold=True) for r_ in (xfr, zTr, onbr, ogTr, hpr, s2r)]
            xf, zTg, onb, ogT, hp, s2 = post
            P.dma("sp", xf.h[:], self.x[rows, :], writes=[xf.d])
            P.dma("sp", zTg.h[:], self.zT_S[g].rearrange("p (h n) -> p h n", h=16)[:, :, cs], reads=[self.sdep("zT", g)], writes=[zTg.d])
            self.actf(sm.h[:, 40:48], sm.h[:, 32:40], AF.Sqrt, [ssd], [ssd], bias=self.eps6.h[:, 0:1], scale=1.0 / 256.0)
            yield "s"
            P.add("dve", lambda e: e.reciprocal(sm.h[:, 40:48], sm.h[:, 40:48]), [ssd], [ssd])
            for h in range(8):
                self.stt("dve", onb.h[:, 2 * h:2 * h + 2, :].rearrange("p a n -> p (a n)"), osb.h[:, h, :], sm.h[:, 40 + h:41 + h], self.normg,
                         ALU.mult, ALU.mult, [osb.d, ssd, self.vb.d], [onb.d])
            rel(osb)
            yield "s"
            for hf in range(2):
                bk = self.bank(hold=True)
                bkb = bk.h[:].bitcast(BF16)
                for cc in range(8):
                    c = hf * 8 + cc
                    self.tr(bkb[:, cc * 128:(cc + 1) * 128], onb.h[:, c, :], self.identB, [onb.d, self.cstb.d], [bk.d])
                yield "s"
                self.tt("dve", ogT.h[:, hf * 8:hf * 8 + 8, :], bkb.rearrange("p (c n) -> p c n", c=8), zTg.h[:, hf * 8:hf * 8 + 8, :],
                        ALU.mult, [bk.d, zTg.d], [ogT.d])
                rel(bk)
            yield "s"
            yb = [self.bank(hold=True), self.bank(hold=True)]
            for half in range(2):
                for c in range(16):
                    self.mm(yb[half].h[:], ogT.h[:, c, :], wo.h[:, c, half * 512:(half + 1) * 512], c == 0, c == 15,
                            [ogT.d] + wod[half * 4:half * 4 + 4], [yb[half].d])
            yield "s"
            for _ in self.resid_ln_g(0, xf, yb, self.h1_S[rows, :], self.sdep(id(self.h1_S), T), hp, s2):
                yield "s"
            rel(sm, *post)

        active = [[0, tile_gen(0)]]
        nxt = 1
        while active:
            for ent in list(active):
                try:
                    tag = next(ent[1])
                except StopIteration:
                    active.remove(ent)
                    continue
                if tag == "recur" and nxt < NT and ent[0] == nxt - 1:
                    active.append([nxt, tile_gen(nxt)])
                    nxt += 1

    def phaseC(self):
        P, NG = self.P, self.NG
        wC = self.sb("wC", [128, 8, 5120], BF16)
        wd = [Dep() for _ in range(10)]
        stg = self.ring("wstgC", 2, [128, 8, 256])
        wd = [Dep() for _ in range(20)]
        self.load_weight(wC, wd[0:12], self.kv_w, 3072, stg, blk=256)

        class _Off:
            pass
        nbk = 8
        for cbk in range(nbk):
            c0 = cbk * 256
            s = stg.next()
            P.dma("sp", s.h[:], self.aw_in[:, c0:c0 + 256].rearrange("(k p) n -> p k n", p=128), writes=[s.d])
            self.copy(("dve", "act", "pool")[cbk % 3], wC.h[:, :, 3072 + c0:3072 + c0 + 256], s.h[:], [s.d], [wd[12 + cbk]])
        WC = 6
        xfr = self.ring("xfC", 2, [128, 1024])
        hTs = [self.sb("hTC", [128, 8, 512], BF16, nd=4) for _ in range(2)]
        rps = [self.sb("ropeC", [128, 4, 512]) for _ in range(2)]
        KT1 = self.sb("KTgC", [128, 12, 512], BF16)
        QT1 = self.sb("QTgC", [128, 12, 512], BF16)
        KTs, QTs = [KT1, KT1], [QT1, QT1]
        rawr = self.ring("rawC", WC, [128, 512], BF16)
        t1r = self.ring("t1C", WC, [128, 512])
        t2r = self.ring("t2C", WC, [128, 512])
        vtr = self.ring("vtC", 3, [128, 1536], BF16)
        ztr = self.ring("ztC", 3, [128, 512], BF16)
        done = [0] * NG

        def xl(g):
            self.load_xT(self.h1_S, g, xfr, hTs[g % 2])
            rp = rps[g % 2]
            P.dma("sp", rp.h[:], self.rope[:, :, g * 512:(g + 1) * 512].rearrange("a p n -> p a n"), writes=[rp.d])
            yield

        def head(g, qk, hd):
            hT, rp = hTs[g % 2], rps[g % 2]
            dst = (QTs if qk else KTs)[g % 2]
            col0 = (3072 if qk else 0) + hd * 128
            pb = self.bank(hold=True)
            for k in range(8):
                self.mm(pb.h[:], wC.h[:, k, col0:col0 + 128], hT.h[:, k, :], k == 0, k == 7, [wd[col0 // 256]] + hT.ds, [pb.d])
            yield
            raw, t1 = rawr.next(hold=True), t1r.next(hold=True)
            self.copy("act", raw.h[:], pb.h[:], [pb.d], [raw.d])
            self.tt("dve", t1.h[:], pb.h[:], rp.h[:, 2 * qk, :], ALU.mult, [pb.d, rp.d], [t1.d])
            rel(pb)
            yield
            p2 = self.bank(hold=True)
            self.mm(p2.h[:], self.permB, raw.h[:], True, True, [self.cstb.d, raw.d], [p2.d])
            rel(raw)
            yield
            t2 = t2r.next(hold=True)
            self.tt("dve", t2.h[:], p2.h[:], rp.h[:, 2 * qk + 1, :], ALU.mult, [p2.d, rp.d], [t2.d])
            rel(p2)
            yield
            self.tt("pool", dst.h[:, hd, :], t1.h[:], t2.h[:], ALU.add, [t1.d, t2.d], [dst.d])
            rel(t1, t2)
            done[g] += 1

        def vz(g, t):
            hT = hTs[g % 2]
            T = g * 4 + t
            rows = slice(T * 128, (T + 1) * 128)
            pbs = [self.bank(hold=True) for _ in range(2)]
            for cg in range(2):
                c0 = 1536 + cg * 512
                for k in range(8):
                    self.mm(pbs[cg].h[:], hT.h[:, k, t * 128:(t + 1) * 128], wC.h[:, k, c0:c0 + 512], k == 0, k == 7, [hT.ds[t], wd[c0 // 256], wd[c0 // 256 + 1]], [pbs[cg].d])
            yield
            vt = vtr.next(hold=True)
            for cg in range(2):
                self.copy("act" if cg % 2 else "dve", vt.h[:, cg * 512:(cg + 1) * 512], pbs[cg].h[:], [pbs[cg].d], [vt.d])
            rel(*pbs)
            yield
            pbs = [self.bank(hold=True) for _ in range(2)]
            for k in range(8):
                self.mm(pbs[0].h[:], hT.h[:, k, t * 128:(t + 1) * 128], wC.h[:, k, 2560:3072], k == 0, k == 7, [hT.ds[t], wd[10], wd[11]], [pbs[0].d])
            for k in range(8):
                self.mm(pbs[1].h[:], hT.h[:, k, t * 128:(t + 1) * 128], wC.h[:, k, 4608:5120], k == 0, k == 7, [hT.ds[t], wd[18], wd[19]], [pbs[1].d])
            yield
            zt = ztr.next(hold=True)
            self.copy("dve", vt.h[:, 1024:1536], pbs[0].h[:], [pbs[0].d], [vt.d])
            self.actf(zt.h[:], pbs[1].h[:], AF.Silu, [pbs[1].d], [zt.d])
            rel(*pbs)
            P.dma("pool", self.V_S[rows, :], vt.h[:], reads=[vt.d], writes=[self.sdep("V", T)])
            P.dma("pool", self.Z_S[rows, :], zt.h[:], reads=[zt.d], writes=[self.sdep("Z", T)])
            rel(vt, zt)

        def fin(g):
            while done[g] < 24:
                yield
            P.dma("pool", self.KT_S[g].rearrange("p (h n) -> p h n", h=12), KTs[g % 2].h[:], reads=[KTs[g % 2].d], writes=[self.sdep("KT", g)])
            P.dma("pool", self.QT_S[g].rearrange("p (h n) -> p h n", h=12), QTs[g % 2].h[:], reads=[QTs[g % 2].d], writes=[self.sdep("QT", g)])

        def tasks():
            for g in range(NG):
                lst = [head(g, qk, hd) for qk in range(2) for hd in range(12)]
                for t in range(4):
                    lst.insert(6 * t + 3 + t, vz(g, t))
                for i, tk in enumerate(lst):
                    yield tk
                    if i == 0 and g + 1 < NG:
                        yield xl(g + 1)
                yield fin(g)

        for _ in xl(0):
            pass
        pipeline(tasks(), WC)

    def phaseD(self):
        P, S = self.P, self.S
        for gi, d in enumerate((1, 4, 16)):
            if gi > 0:
                self.new_phase()
            WD = 8
            smr = self.ring("smD", 4, [128, 256])
            pbr = self.ring("pbD", 4, [128, 256], BF16)
            pTr = self.ring("pTD", 4, [128, 256], BF16)
            osr = self.ring("osD", 4, [128, 4, 128])
            sr = self.ring("sD", 4, [128, 16], nd=4)
            W = max(512, 128 * d)
            ngrp = W // 512
            nbuf = S // W
            spb = W // (128 * d)
            Qr = self.ring(f"QD{gi}", 3, [128, 4, W], BF16)
            Kr = self.ring(f"KD{gi}", 4, [128, 4, W], BF16)
            Vr = self.ring(f"VD{gi}", 4, [128, W // 128, 512], BF16)
            bufs = {}

            def getbuf(bi, Qr=Qr, Kr=Kr, Vr=Vr, bufs=bufs, ngrp=ngrp, spb=spb, W=W, d=d, gi=gi):
                if bi in bufs:
                    return bufs[bi]
                q, k, v = Qr.next(), Kr.next(), Vr.next()
                for j in range(ngrp):
                    gg = bi * ngrp + j
                    P.dma("sp", q.h[:, :, j * 512:(j + 1) * 512],
                          self.QT_S[gg].rearrange("p (h n) -> p h n", h=12)[:, gi * 4:gi * 4 + 4, :], reads=[self.sdep("QT", gg)], writes=[q.d])
                    P.dma("sp", k.h[:, :, j * 512:(j + 1) * 512],
                          self.KT_S[gg].rearrange("p (h n) -> p h n", h=12)[:, gi * 4:gi * 4 + 4, :], reads=[self.sdep("KT", gg)], writes=[k.d])
                for sp in range(spb):
                    base = bi * W + sp * 128 * d
                    rd = [self.sdep("V", T) for T in range(base // 128, (base + 128 * d) // 128)]
                    P.dma("sp", v.h[:, sp * d:(sp + 1) * d, :],
                          self.V_S[base:base + 128 * d, gi * 512:(gi + 1) * 512].rearrange("(j r) c -> j r c", r=d), reads=rd, writes=[v.d])
                bufs[bi] = (q, k, v)
                return bufs[bi]

            nspan = S // (128 * d)

            class Blk:
                pass

            def head(bl, hh, d=d):
                n, r, sm, osb = bl.n, bl.r, bl.sm, bl.osb
                q, kc, vc, off, so = bl.q, bl.kc, bl.vc, bl.off, bl.so
                lo = 0 if n > 0 else 128
                sd = sm.ds[hh]
                P.add("pool", lambda e: e.memset(sm.h[:, hh:hh + 1], 0.0), [sd], [sd])
                sc = self.bank(hold=True)
                qs = q.h[:, hh, off + r:off + 128 * d:d]
                if n > 0:
                    self.mm(sc.h[:, 0:128], qs, bl.kp.h[:, hh, bl.poff + r:bl.poff + 128 * d:d], True, True, [q.d, bl.kp.d], [sc.d])
                self.mm(sc.h[:, 128:256], qs, kc.h[:, hh, off + r:off + 128 * d:d], True, True, [q.d, kc.d], [sc.d])
                yield
                smb = smr.next(hold=True)
                self.tt("dve", smb.h[:, lo:256], sc.h[:, lo:256], self.maskA[:, lo:256], ALU.add, [sc.d, self.cst.d], [smb.d])
                rel(sc)
                P.add("dve", lambda e: e.tensor_reduce(sm.h[:, 4 + hh:5 + hh], smb.h[:, lo:256], AX.X, ALU.max, negate=True), [smb.d, sd], [sd])
                yield
                pbf = pbr.next(hold=True)
                self.actf(pbf.h[:, lo:256], smb.h[:, lo:256], AF.Exp, [smb.d, sd], [pbf.d, sd],
                          bias=sm.h[:, 4 + hh:5 + hh], scale=1.0, accum_out=sm.h[:, hh:hh + 1])
                rel(smb)
                yield
                tbk = self.bank(hold=True)
                tbb = tbk.h[:].bitcast(BF16)
                if n > 0:
                    self.tr(tbb[:, 0:128], pbf.h[:, 0:128], self.identB, [pbf.d, self.cstb.d], [tbk.d])
                self.tr(tbb[:, 128:256], pbf.h[:, 128:256], self.identB, [pbf.d, self.cstb.d], [tbk.d])
                rel(pbf)
                yield
                pT = pTr.next(hold=True)
                self.copy("act", pT.h[:, lo:256], tbb[:, lo:256], [tbk.d], [pT.d])
                rel(tbk)
                P.add("dve", lambda e: e.reciprocal(sm.h[:, 8 + hh:9 + hh], sm.h[:, hh:hh + 1]), [sd], [sd])
                yield
                ob = self.bank(hold=True)
                if n > 0:
                    self.mm(ob.h[:, 0:128], pT.h[:, 0:128], bl.vp.h[:, bl.pso * d + r, hh * 128:(hh + 1) * 128], True, False, [pT.d, bl.vp.d], [ob.d])
                self.mm(ob.h[:, 0:128], pT.h[:, 128:256], vc.h[:, so * d + r, hh * 128:(hh + 1) * 128], n == 0, True, [pT.d, vc.d], [ob.d])
                rel(pT)
                yield
                P.add("dve", lambda e: e.tensor_scalar(osb.h[:, hh, :], ob.h[:, 0:128], sm.h[:, 8 + hh:9 + hh], None, ALU.mult),
                      [ob.d, sd], [osb.d])
                rel(ob)
                bl.done += 1

            def fin(bl, d=d, gi=gi):
                while bl.done < 4:
                    yield
                sm, osb, r = bl.sm, bl.osb, bl.r
                self.actf(sm.h[:, 12:16], sm.h[:, 0:4], AF.Ln, sm.ds, sm.ds)
                yield
                self.tt("dve", sm.h[:, 12:16], sm.h[:, 12:16], sm.h[:, 4:8], ALU.subtract, sm.ds, sm.ds)
                base = bl.n * 128 * d
                tl = list(range(base // 128, (base + 128 * d) // 128))
                P.dma("pool", self.O_S[gi, base:base + 128 * d, :].rearrange("(i r) c -> i r c", r=d)[:, r, :],
                      osb.h[:].rearrange("p h n -> p (h n)"), reads=[osb.d], writes=[self.sdep("O", gi, T, r) for T in tl])
                P.dma("pool", self.L_S[gi, base:base + 128 * d, :].rearrange("(i r) c -> i r c", r=d)[:, r, :],
                      sm.h[:, 12:16], reads=sm.ds, writes=[self.sdep("L", gi, T, r) for T in tl])
                rel(sm, osb)

            def tasks(d=d, spb=spb, nspan=nspan, nbuf=nbuf):
                for n in range(nspan):
                    bi, so = n // spb, n % spb
                    q, kc, vc = getbuf(bi)
                    if so == 0 and bi + 1 < nbuf:
                        getbuf(bi + 1)
                    for r in range(d):
                        bl = Blk()
                        bl.n, bl.r, bl.q, bl.kc, bl.vc, bl.so, bl.off, bl.done = n, r, q, kc, vc, so, so * 128 * d, 0
                        if n > 0:
                            pbi, bl.pso = (n - 1) // spb, (n - 1) % spb
                            _, bl.kp, bl.vp = getbuf(pbi)
                            bl.poff = bl.pso * 128 * d
                        bl.osb, bl.sm = osr.next(hold=True), sr.next(hold=True)
                        for hh in range(4):
                            yield head(bl, hh)
                        yield fin(bl)

            pipeline(tasks(), WD)

    def phaseE(self):
        P, NT = self.P, self.NT
        wo = self.sb("woE", [128, 4, 1024], BF16)
        wod = [Dep() for _ in range(2)]
        stg = self.ring("wstgE", 2, [128, 4, 512])
        self.load_weight(wo, wod, self.aw_out, 1024, stg)
        lnv = self.lnv = self.sb("lnvE", [128, 2048])
        P.dma("sp", lnv.h[:], self.vecs[:, 2048:4096].partition_broadcast(128), writes=[lnv.d])
        WE = 4
        Or = self.ring("OE", WE + 1, [128, 3, 512])
        Lr = self.ring("LE", WE + 1, [128, 3, 4])
        zr = self.ring("zE", WE + 1, [128, 512], BF16)
        xfr = self.ring("xfE", WE + 1, [128, 1024])
        smr = self.ring("smE", WE + 1, [128, 32])
        tmr = self.ring("tmE", WE + 1, [128, 3, 512])
        ogr = self.ring("ogE", WE + 1, [128, 512], BF16)
        oTr = self.ring("oTE", WE + 1, [128, 4, 128], BF16)
        hpr = self.ring("hpE", WE + 1, [128, 1024])
        s2r = self.ring("s2E", WE + 1, [128, 16])

        def tile(T):
            rows = slice(T * 128, (T + 1) * 128)
            bufs = [r_.next(hold=True) for r_ in (Or, Lr, zr, xfr, smr, tmr, ogr, oTr, hpr, s2r)]
            O, L, z, xf, sm, tm, og, oT, hp, s2 = bufs
            for gi, d in enumerate((1, 4, 16)):
                P.dma("sp", O.h[:, gi, :], self.O_S[gi, rows, :], reads=[self.sdep("O", gi, T, r) for r in range(d)], writes=[O.d])
                P.dma("sp", L.h[:, gi, :], self.L_S[gi, rows, :], reads=[self.sdep("L", gi, T, r) for r in range(d)], writes=[L.d])
            P.dma("sp", z.h[:], self.Z_S[rows, :], reads=[self.sdep("Z", T)], writes=[z.d])
            P.dma("sp", xf.h[:], self.h1_S[rows, :], reads=[self.sdep(id(self.h1_S), T)], writes=[xf.d])
            yield
            self.tt("dve", sm.h[:, 0:4], L.h[:, 0, :], L.h[:, 1, :], ALU.max, [L.d], [sm.d])
            self.tt("dve", sm.h[:, 0:4], sm.h[:, 0:4], L.h[:, 2, :], ALU.max, [L.d, sm.d], [sm.d])
            e3 = sm.h[:, 4:16].rearrange("p (g h) -> p g h", g=3)
            self.tt("dve", e3, L.h[:], sm.h[:, 0:4].unsqueeze(1).to_broadcast([128, 3, 4]), ALU.subtract, [L.d, sm.d], [sm.d])
            yield
            self.actf(sm.h[:, 4:16], sm.h[:, 4:16], AF.Exp, [sm.d], [sm.d])
            yield
            self.tt("dve", sm.h[:, 16:20], sm.h[:, 4:8], sm.h[:, 8:12], ALU.add, [sm.d], [sm.d])
            self.tt("dve", sm.h[:, 16:20], sm.h[:, 16:20], sm.h[:, 12:16], ALU.add, [sm.d], [sm.d])
            P.add("dve", lambda e: e.reciprocal(sm.h[:, 20:24], sm.h[:, 16:20]), [sm.d], [sm.d])
            self.tt("dve", e3, e3, sm.h[:, 20:24].unsqueeze(1).to_broadcast([128, 3, 4]), ALU.mult, [sm.d], [sm.d])
            yield
            for gi in range(3):
                self.tt("dve", tm.h[:, gi, :].rearrange("p (h n) -> p h n", h=4), O.h[:, gi, :].rearrange("p (h n) -> p h n", h=4),
                        sm.h[:, 4 + gi * 4:8 + gi * 4].unsqueeze(2).to_broadcast([128, 4, 128]), ALU.mult, [O.d, sm.d], [tm.d])
            yield
            self.tt("pool", tm.h[:, 0, :], tm.h[:, 0, :], tm.h[:, 1, :], ALU.add, [tm.d], [tm.d])
            self.tt("pool", tm.h[:, 0, :], tm.h[:, 0, :], tm.h[:, 2, :], ALU.add, [tm.d], [tm.d])
            yield
            self.tt("dve", og.h[:], tm.h[:, 0, :], z.h[:], ALU.mult, [tm.d, z.d], [og.d])
            yield
            bk = self.bank(hold=True)
            bkb = bk.h[:].bitcast(BF16)
            for c in range(4):
                self.tr(bkb[:, c * 128:(c + 1) * 128], og.h[:, c * 128:(c + 1) * 128], self.identB, [og.d, self.cstb.d], [bk.d])
            yield
            self.copy("act", oT.h[:], bkb[:, 0:512].rearrange("p (c n) -> p c n", c=4), [bk.d], [oT.d])
            rel(bk)
            yield
            yb = [self.bank(hold=True), self.bank(hold=True)]
            for half in range(2):
                for c in range(4):
                    self.mm(yb[half].h[:], oT.h[:, c, :], wo.h[:, c, half * 512:(half + 1) * 512], c == 0, c == 3, [oT.d, wod[half]], [yb[half].d])
            yield
            yield from self.resid_ln_g(1, xf, yb, self.out[rows, :], Dep(), hp, s2)
            rel(*bufs)

        pipeline((tile(T) for T in range(NT)), WE)

    def small_consts(self):
        P = self.P
        for name, val in (("eps5", 1e-5), ("eps6", 1e-6), ("eps6q", 128e-6), ("one", 1.0)):
            b = self.sb(name, [128, 1])
            P.add("pool", lambda e, b=b, val=val: e.memset(b.h[:], val), [], [b.d])
            setattr(self, name, b)

    def build(self, phases="AGNBCDE"):
        self.load_consts()
        self.small_consts()
        self.sb_glob = self.sb_ptr
        for ph in phases:
            self.new_phase()
            getattr(self, "phase" + ph)()
            self.sb_use = getattr(self, "sb_use", {})
            self.sb_use[ph] = self.sb_ptr
        self.P.emit()
        return self.nc


def host_consts(S):
    p = np.arange(128)[:, None]
    f = np.arange(128)[None, :]
    c = np.zeros((128, 1024), np.float32)
    c[:, 0:128] = (p == f)
    c[:, 128:256] = (p <= f)
    c[:, 256:384] = np.where(f >= p, 0.0, NEG)
    c[:, 384:512] = np.where(f < p, 0.0, NEG)
    c[:, 512:640] = 1.0
    c[:, 640:768] = (p == (f + 64) % 128)
    i = np.arange(128)[:, None]
    j = np.arange(256)[None, :]
    off = i + 128 - j
    c[:, 768:1024] = np.where((off >= 0) & (off <= 128), 0.0, NEG)
    inv = 1.0 / (10000.0 ** (np.arange(0, 128, 2, dtype=np.float32) / 128))
    ang = np.arange(S, dtype=np.float32)[:, None] * inv[None, :].astype(np.float32)
    ang = np.concatenate([ang, ang], -1).astype(np.float32)
    cos = np.cos(ang).T.astype(np.float32)
    sin = np.sin(ang).T.astype(np.float32)
    sgn = np.where(np.arange(128) < 64, -1.0, 1.0).astype(np.float32)[:, None]
    sc = np.float32(128 ** -0.5)
    rope = np.stack([cos, sin * sgn, cos * sc, sin * sgn * sc]).astype(np.float32)
    return c, rope


_CACHE = {}


def make_inputs(S, x, ln_g, ln_b, gdn_w_in, gdn_conv_w, gdn_a_log, gdn_dt_bias, gdn_norm_g, gdn_w_out, kv_w, att_w_in, att_w_out):
    c, rope = host_consts(S)
    vecs = np.concatenate([ln_g[0], ln_b[0], ln_g[1], ln_b[1], gdn_norm_g[0], gdn_a_log[0], gdn_dt_bias[0]]).astype(np.float32)[None, :]
    cw = np.ascontiguousarray(gdn_conv_w[0].reshape(4, 32, 128).transpose(2, 1, 0)).reshape(128, 128)
    shared = dict(w_in=np.ascontiguousarray(gdn_w_in[0]), w_out=np.ascontiguousarray(gdn_w_out[0]), kv_w=np.ascontiguousarray(kv_w),
                  aw_in=np.ascontiguousarray(att_w_in[0]), aw_out=np.ascontiguousarray(att_w_out[0]), cw=cw, vecs=vecs, consts=c, rope=rope)
    return [dict(shared, x=np.ascontiguousarray(x[b])) for b in range(x.shape[0])]


def kernel(**inputs):
    x = np.asarray(inputs["x"], np.float32)
    Bn, S, _ = x.shape
    args = {k: np.asarray(v, np.float32) for k, v in inputs.items()}
    in_maps = make_inputs(S, **args)
    if S not in _CACHE:
        _CACHE[S] = K(S).build()
    nc = _CACHE[S]
    res = run_bass_kernel_spmd(nc, in_maps, core_ids=list(range(Bn)))
    return np.stack([np.asarray(r["out"], np.float32) for r in res.results], 0)
```

```python
import contextlib
import math
import numpy as np
import concourse.bass as bass
import concourse.mybir as mybir
from concourse.bass_utils import run_bass_kernel_spmd

F32 = mybir.dt.float32
BF16 = mybir.dt.bfloat16
AF = mybir.ActivationFunctionType
ALU = mybir.AluOpType
AX = mybir.AxisListType
NEG = -30000.0
import os
BSTOP = int(os.environ.get('BSTOP', '99'))
BSUB = int(os.environ.get('BSUB', '99'))


class Dep:
    __slots__ = ("w", "r", "x")

    def __init__(self):
        self.w = None
        self.r = {}
        self.x = False


class Prog:
    ENGS = ("pe", "act", "dve", "pool", "sp")

    def __init__(self, nc, n_dma_sems=32, epoch=4000):
        self.nc = nc
        self.ops = []
        self.n_dma_sems = n_dma_sems
        self.epoch = epoch

    def add(self, eng, fn, reads=(), writes=(), dma=False):
        idx = len(self.ops)
        deps = set()
        if any(t.x for t in reads):
            writes = list(writes) + [t for t in reads if t.x and t not in writes]
            reads = [t for t in reads if not t.x]
        for t in reads:
            if t.w is not None:
                deps.add(t.w)
        for t in writes:
            if t.w is not None:
                deps.add(t.w)
            deps.update(t.r.values())
        rk = ("dma", idx) if dma else eng
        for t in reads:
            t.r[rk] = idx
        for t in writes:
            t.w = idx
            t.r = {}
        deps.discard(idx)
        self.ops.append((eng, fn, deps, dma))
        return idx

    def dma(self, eng, out, in_, reads=(), writes=()):
        return self.add(eng, lambda e: e.dma_start(out=out, in_=in_), reads, writes, dma=True)

    def barrier(self):
        last = {}
        dmas = set()
        for i, (eng, fn, deps, is_dma) in enumerate(self.ops):
            if is_dma:
                dmas.add(i)
            elif fn is not None:
                last[eng] = i
        deps = set(last.values()) | dmas
        for e in self.ENGS:
            self.ops.append((e, None, set(deps), False))

    def emit(self):
        nc = self.nc
        ops = self.ops
        n = len(ops)
        signal = [False] * n
        for i, (eng, fn, deps, is_dma) in enumerate(ops):
            for d in deps:
                deng, _, _, ddma = ops[d]
                if ddma:
                    continue
                if deng != eng or is_dma or eng != "pe":
                    signal[d] = True
        cnt = {e: 0 for e in self.ENGS}
        sig_of = [None] * n
        dma_of = [None] * n
        ndma = 0
        for i, (eng, fn, deps, is_dma) in enumerate(ops):
            if is_dma:
                dma_of[i] = (ndma % self.n_dma_sems, 16 * (ndma // self.n_dma_sems + 1))
                ndma += 1
            elif signal[i]:
                c = cnt[eng]
                sig_of[i] = (eng, c // self.epoch, c % self.epoch + 1)
                cnt[eng] = c + 1
        n_epochs = {e: (cnt[e] + self.epoch - 1) // self.epoch for e in self.ENGS}
        self.stats = dict(n_ops=n, n_dma=ndma, signals=dict(cnt))
        with contextlib.ExitStack() as st:
            esems = {}
            for e in self.ENGS:
                for k in range(n_epochs[e]):
                    esems[(e, k)] = st.enter_context(nc.semaphore(f"s_{e}_{k}"))
            dsems = [st.enter_context(nc.semaphore(f"s_dma_{k}")) for k in range(min(self.n_dma_sems, max(ndma, 1)))]
            block = st.enter_context(nc.Block())

            def make(engname):
                def body(e):
                    known = {}

                    def wait(sem, key, val):
                        if known.get(key, 0) < val:
                            e.wait_ge(sem, val)
                            known[key] = val

                    for i, (eng, fn, deps, is_dma) in enumerate(ops):
                        if eng != engname:
                            continue
                        need = {}
                        for d in deps:
                            if ops[d][3]:
                                k, v = dma_of[d]
                                key = ("d", k)
                                if need.get(key, (None, 0))[1] < v:
                                    need[key] = (dsems[k], v)
                            else:
                                if sig_of[d] is None:
                                    continue
                                if ops[d][0] == "pe" and eng == "pe" and not is_dma:
                                    continue
                                se, ep, v = sig_of[d]
                                key = (se, ep)
                                if need.get(key, (None, 0))[1] < v:
                                    need[key] = (esems[key], v)
                        if is_dma:
                            k, v = dma_of[i]
                            if v > 16:
                                key = ("d", k)
                                if need.get(key, (None, 0))[1] < v - 16:
                                    need[key] = (dsems[k], v - 16)
                        for key, (sem, v) in need.items():
                            wait(sem, key, v)
                        if fn is None:
                            continue
                        inst = fn(e)
                        if is_dma:
                            k, v = dma_of[i]
                            inst.then_inc(dsems[k], 16)
                        elif sig_of[i] is not None:
                            se, ep, v = sig_of[i]
                            inst.then_inc(esems[(se, ep)], 1)
                    if engname == "sp":
                        last = {}
                        for i in range(n):
                            if dma_of[i] is not None:
                                k, v = dma_of[i]
                                last[k] = max(last.get(k, 0), v)
                        for k, v in last.items():
                            wait(dsems[k], ("d", k), v)
                return body

            block.tensor(make("pe"))
            block.scalar(make("act"))
            block.vector(make("dve"))
            block.gpsimd(make("pool"))
            block.sync(make("sp"))


class B:
    def __init__(self, h, nd=1):
        self.h = h
        self.d = Dep()
        self.ds = [Dep() for _ in range(nd)] if nd > 1 else [self.d]
        self.held = False


class Ring:
    def __init__(self, items):
        self.items = items
        self.i = 0

    def next(self, hold=False):
        n = len(self.items)
        for _ in range(n):
            b = self.items[self.i % n]
            self.i += 1
            if not b.held:
                b.held = hold
                return b
        raise AssertionError("ring exhausted (all buffers held)")


def rel(*bs):
    for b in bs:
        b.held = False


def pipeline(tasks, width):
    tasks = iter(tasks)
    active = []
    done = False
    while True:
        if len(active) < width and not done:
            try:
                active.append(next(tasks))
            except StopIteration:
                done = True
        if not active:
            if done:
                break
            continue
        for g in list(active):
            try:
                next(g)
            except StopIteration:
                active.remove(g)


D_MODEL = 1024
GH = 8
CW_A = 6160
ALPHA = 4.0 ** 0.25


class K:
    def __init__(self, S, dbg=False):
        self.S = S
        self.NT = S // 128
        self.NG = S // 512
        self.dbg = dbg
        self.nc = nc = bass.Bass("TRN2", target_bir_lowering=False)
        self.P = Prog(nc)
        self.st = contextlib.ExitStack()
        self.cnt = 0
        self.sb_ptr = 16512
        NT, NG = self.NT, self.NG

        def din(name, shape, dt=F32):
            return nc.dram_tensor(name, shape, dt, kind="ExternalInput").ap()

        self.x = din("x", [S, 1024])
        self.w_in = din("w_in", [1024, CW_A])
        self.w_out = din("w_out", [2048, 1024])
        self.kv_w = din("kv_w", [1024, 3072])
        self.aw_in = din("aw_in", [1024, 2048])
        self.aw_out = din("aw_out", [512, 1024])
        self.cwd = din("cw", [128, 32 * 4])
        self.vecs = din("vecs", [1, 4 * 1024 + 256 + 16])
        self.consts = din("consts", [128, 1024])
        self.rope = din("rope", [4, 128, S])
        self.out = nc.dram_tensor("out", [S, 1024], F32, kind="ExternalOutput").ap()
        kind = "ExternalOutput" if dbg else "Internal"

        def scr(name, shape, dt):
            return nc.dram_tensor(name, shape, dt, kind=kind).ap()

        self.qT_S = scr("qT_S", [NG, 128, 8 * 512], BF16)
        self.kT_S = scr("kT_S", [NG, 128, 8 * 512], BF16)
        self.zT_S = scr("zT_S", [NG, 128, 16 * 512], BF16)
        self.ktok_S = scr("ktok_S", [S, 1024], BF16)
        self.vtok_S = scr("vtok_S", [S, 2048], BF16)
        self.bg_S = scr("bg_S", [128, NT * 16], F32)
        self.bgr_S = scr("bgr_S", [128, NT * 16], F32)
        self.Tt_S = scr("Tt_S", [NT, 128, 1024], BF16)
        self.DT_S = scr("DT_S", [NT, 128, 1024], BF16)
        self.gb_S = scr("gb_S", [NT, 128, 1024], BF16)
        self.sm_S = scr("sm_S", [NT, 128, 64], F32)
        self.h1_S = scr("h1_S", [S, 1024], F32)
        self.QT_S = scr("QT_S", [NG, 128, 12 * 512], BF16)
        self.KT_S = scr("KT_S", [NG, 128, 12 * 512], BF16)
        self.V_S = scr("V_S", [S, 1536], BF16)
        self.Z_S = scr("Z_S", [S, 512], BF16)
        self.O_S = scr("O_S", [3, S, 512], F32)
        self.L_S = scr("L_S", [3, S, 4], F32)
        self.sd = {}
        self.banks = Ring([B(self.st.enter_context(nc.psum_tensor(f"bank{i}", [128, 512], F32))) for i in range(8)])
        for b in self.banks.items:
            b.d.x = True

    def sdep(self, *key):
        if key not in self.sd:
            self.sd[key] = Dep()
        return self.sd[key]

    def sb(self, name, shape, dt=F32, nd=1):
        self.cnt += 1
        nb = int(np.prod(shape[1:])) * (2 if dt == BF16 else 4)
        nb = (nb + 31) // 32 * 32
        off = self.sb_ptr
        self.sb_ptr += nb
        assert self.sb_ptr <= 229344, (name, self.sb_ptr)
        return B(self.nc.alloc_sbuf_tensor_at(f"{name}_{self.cnt}", shape, dt, offset=off), nd)

    def new_phase(self):
        self.sb_ptr = self.sb_glob
        self.P.barrier()

    def ring(self, name, n, shape, dt=F32, nd=1):
        return Ring([self.sb(name, shape, dt, nd) for _ in range(n)])

    def bank(self, hold=False):
        return self.banks.next(hold)

    def copy(self, eng, out, in_, reads, writes):
        if eng == "act":
            self.P.add("act", lambda e: e.copy(out, in_), reads, writes)
        else:
            self.P.add(eng, lambda e: e.tensor_copy(out, in_), reads, writes)

    def tt(self, eng, out, a, b, op, reads, writes):
        self.P.add(eng, lambda e: e.tensor_tensor(out, a, b, op), reads, writes)

    def stt(self, eng, out, a, sc, b, op0, op1, reads, writes):
        self.P.add("dve", lambda e: e.scalar_tensor_tensor(out, a, sc, b, op0, op1), reads, writes)

    def actf(self, out, in_, func, reads, writes, **kw):
        self.P.add("act", lambda e: e.activation(out, in_, func, **kw), reads, writes)

    def mm(self, out, lhsT, rhs, start, stop, reads, writes):
        self.P.add("pe", lambda e: e.matmul(out, lhsT, rhs, start=start, stop=stop), reads, writes)

    def tr(self, out, in_, ident, reads, writes):
        self.P.add("pe", lambda e: e.transpose(out, in_, ident), reads, writes)

    def load_consts(self):
        P = self.P
        c = self.cst = self.sb("cst", [128, 1024])
        P.dma("sp", c.h[:], self.consts, writes=[c.d])
        self.identF = c.h[:, 0:128]
        self.Umat = c.h[:, 128:256]
        self.NEGU = c.h[:, 256:384]
        self.NEGL = c.h[:, 384:512]
        self.onesF = c.h[:, 512:640]
        self.maskA = c.h[:, 768:1024]
        cb = self.cstb = self.sb("cstb", [128, 384], BF16)
        self.copy("dve", cb.h[:, 0:128], c.h[:, 0:128], [c.d], [cb.d])
        self.copy("dve", cb.h[:, 128:256], c.h[:, 640:768], [c.d], [cb.d])
        self.copy("dve", cb.h[:, 256:384], c.h[:, 512:640], [c.d], [cb.d])
        self.onesB = cb.h[:, 256:384]
        self.identB = cb.h[:, 0:128]
        self.permB = cb.h[:, 128:256]

    def load_weight(self, dst, ddeps, src, ncols, stg, blk=512, engs=("dve", "act", "pool")):
        self._kk = src.shape[0] // 128
        nb = (ncols + blk - 1) // blk
        for cbk in range(nb):
            c0 = cbk * blk
            cw = min(blk, ncols - c0)
            s = stg.next()
            kk = self._kk
            self.P.dma("sp", s.h[:, 0:kk, 0:cw], src[:, c0:c0 + cw].rearrange("(k p) n -> p k n", p=128), writes=[s.d])
            self.copy(engs[cbk % len(engs)], dst.h[:, :, c0:c0 + cw], s.h[:, 0:kk, 0:cw], [s.d], [ddeps[cbk]])

    def load_xT(self, src, g, xfr, xT):
        P = self.P
        xfs = []
        for t in range(4):
            xf = xfr.next()
            T = g * 4 + t
            P.dma("sp", xf.h[:], src[T * 128:(T + 1) * 128, :], reads=[self.sdep(id(src), T)] if src is self.h1_S else [], writes=[xf.d])
            for half in range(2):
                pb = self.bank()
                for kk in range(4):
                    k = half * 4 + kk
                    self.tr(pb.h[:, kk * 128:(kk + 1) * 128], xf.h[:, k * 128:(k + 1) * 128], self.identF, [xf.d, self.cst.d], [pb.d])
                self.copy("act" if half else "dve", xT.h[:, half * 4:half * 4 + 4, t * 128:(t + 1) * 128],
                          pb.h[:].rearrange("p (k n) -> p k n", k=4), [pb.d], [xT.ds[t]])
            xfs.append(xf)
        return xfs

    def resid_ln(self, layer, xf, ybanks, dst_ap, dst_dep, hp, small):
        for _ in self.resid_ln_g(layer, xf, ybanks, dst_ap, dst_dep, hp, small):
            pass

    def resid_ln_g(self, layer, xf, ybanks, dst_ap, dst_dep, hp, small):
        P = self.P
        for half in range(2):
            bk = ybanks[half]
            self.stt("dve", hp.h[:, half * 512:(half + 1) * 512], xf.h[:, half * 512:(half + 1) * 512], ALPHA, bk.h[:],
                     ALU.mult, ALU.add, [xf.d, bk.d], [hp.d])
        rel(*ybanks)
        stt_ = small.h
        P.add("dve", lambda e: e.bn_stats(stt_[:, 0:6], hp.h[:, 0:512]), [hp.d], [small.d])
        P.add("dve", lambda e: e.bn_stats(stt_[:, 6:12], hp.h[:, 512:1024]), [hp.d], [small.d])
        P.add("dve", lambda e: e.bn_aggr(stt_[:, 12:14], stt_[:, 0:12].rearrange("p (a b) -> p a b", a=2)), [small.d], [small.d])
        yield
        self.actf(stt_[:, 14:15], stt_[:, 13:14], AF.Sqrt, [small.d], [small.d], bias=self.eps5.h[:, 0:1], scale=1.0)
        yield
        P.add("dve", lambda e: e.reciprocal(stt_[:, 15:16], stt_[:, 14:15]), [small.d], [small.d])
        P.add("dve", lambda e: e.tensor_scalar(hp.h[:], hp.h[:], stt_[:, 12:13], stt_[:, 15:16], ALU.subtract, ALU.mult),
              [hp.d, small.d], [hp.d])
        yield
        self.tt("pool", hp.h[:], hp.h[:], self.lnv.h[:, 0:1024], ALU.mult, [hp.d, self.lnv.d], [hp.d])
        self.tt("pool", hp.h[:], hp.h[:], self.lnv.h[:, 1024:2048], ALU.add, [hp.d, self.lnv.d], [hp.d])
        P.dma("pool", dst_ap, hp.h[:], reads=[hp.d], writes=[dst_dep])

    def phaseA(self):
        P, NG = self.P, self.NG
        wA = self.sb("wA", [128, 8, CW_A], BF16)
        stg = self.ring("wstgA", 2, [128, 8, 128])
        wdeps = [Dep() for _ in range(49)]
        self.load_weight(wA, wdeps, self.w_in, CW_A, stg, blk=128)
        cw = self.sb("cw", [128, 32, 4])
        P.dma("sp", cw.h[:], self.cwd.rearrange("p (c j) -> p c j", j=4), writes=[cw.d])
        halo = self.sb("halo", [128, 32, 4], BF16, nd=32)
        P.add("pool", lambda e: e.memset(halo.h[:], 0.0), [], halo.ds)
        W = 6
        xfr = self.ring("xfA", 2, [128, 1024])
        xTs = [self.sb("xTA", [128, 8, 512], BF16, nd=4) for _ in range(2)]
        prer = self.ring("pre", W + 1, [128, 516], BF16)
        dgr = self.ring("dg", 3, [128, 4, 128], BF16)
        slr = self.ring("sl", 3, [128, 512])
        sqr = self.ring("sq", 3, [128, 512], BF16)
        rnr = self.ring("rn", 3, [128, 512])
        r4r = self.ring("r4", 4, [128, 4])
        nhalf = self.sb("nhalf", [128, 4])
        P.add("pool", lambda e: e.memset(nhalf.h[:], -0.5), [], [nhalf.d])
        ocr = self.ring("oc", W + 1, [128, 512], BF16)
        ktokG = self.sb("ktokG", [128, 4, 1024], BF16)
        vtokG = self.sb("vtokG", [128, 4, 2048], BF16)
        bgr = self.ring("bgt", 2, [128, 16])

        def xload(g, xT):
            for t in range(4):
                T = g * 4 + t
                xf = xfr.next()
                P.dma("sp", xf.h[:], self.x[T * 128:(T + 1) * 128, :], writes=[xf.d])
                for half in range(2):
                    pb = self.bank()
                    for kk in range(4):
                        k = half * 4 + kk
                        self.tr(pb.h[:, kk * 128:(kk + 1) * 128], xf.h[:, k * 128:(k + 1) * 128], self.identF, [xf.d, self.cst.d], [pb.d])
                    self.copy("act" if half else "dve", xT.h[:, half * 4:half * 4 + 4, t * 128:(t + 1) * 128],
                              pb.h[:].rearrange("p (k n) -> p k n", k=4), [pb.d], [xT.ds[t]])
                yield
                pb = self.bank()
                for k in range(8):
                    self.mm(pb.h[:, 0:16], xT.h[:, k, t * 128:(t + 1) * 128], wA.h[:, k, 6144:6160], k == 0, k == 7,
                            [xT.ds[t], wdeps[48]], [pb.d])
                bgt = bgr.next()
                self.copy("dve", bgt.h[:], pb.h[:, 0:16], [pb.d], [bgt.d])
                P.dma("pool", self.bgr_S[:, T * 16:(T + 1) * 16], bgt.h[:], reads=[bgt.d], writes=[Dep()])
                yield

        def chunk(g, c, xT):
            pb = self.bank(hold=True)
            for k in range(8):
                self.mm(pb.h[:], wA.h[:, k, c * 128:(c + 1) * 128], xT.h[:, k, :], k == 0, k == 7, [wdeps[c]] + xT.ds, [pb.d])
            yield
            oc = ocr.next(hold=True)
            if c >= 32:
                self.actf(oc.h[:], pb.h[:], AF.Silu, [pb.d], [oc.d])
                rel(pb)
                P.dma("pool", self.zT_S[g, :, (c - 32) * 512:(c - 31) * 512], oc.h[:], reads=[oc.d], writes=[self.sdep("zT", g)])
                rel(oc)
                return
            pre, dg = prer.next(hold=True), dgr.next(hold=True)
            self.copy("pool", pre.h[:, 0:3], halo.h[:, c, 0:3], [halo.ds[c]], [pre.d])
            self.copy("act", pre.h[:, 3:515], pb.h[:], [pb.d], [pre.d])
            rel(pb)
            self.copy("pool", halo.h[:, c, 0:3], pre.h[:, 512:515], [pre.d], [halo.ds[c]])
            self.tt("pool", dg.h[:], self.identF.unsqueeze(1).to_broadcast([128, 4, 128]),
                    cw.h[:, c, :].unsqueeze(2).to_broadcast([128, 4, 128]), ALU.mult, [self.cst.d, cw.d], [dg.d])
            yield
            pb2 = self.bank(hold=True)
            for j in range(4):
                self.mm(pb2.h[:], dg.h[:, j, :], pre.h[:, j:j + 512], j == 0, j == 3, [dg.d, pre.d], [pb2.d])
            rel(pre, dg)
            yield
            if c >= 16:
                self.actf(oc.h[:], pb2.h[:], AF.Silu, [pb2.d], [oc.d])
                rel(pb2)
                yield
                tb = self.bank(hold=True)
                tbb = tb.h[:].bitcast(BF16)
                for t in range(4):
                    self.tr(tbb[:, t * 128:(t + 1) * 128], oc.h[:, t * 128:(t + 1) * 128], self.identB, [oc.d, self.cstb.d], [tb.d])
                yield
                cc = c - 16
                self.copy("dve", vtokG.h[:, :, cc * 128:(cc + 1) * 128], tbb[:, 0:512].rearrange("p (t n) -> p t n", t=4), [tb.d], [vtokG.d])
                rel(tb, oc)
                return
            sl, sq = slr.next(hold=True), sqr.next(hold=True)
            self.actf(sl.h[:], pb2.h[:], AF.Silu, [pb2.d], [sl.d])
            rel(pb2)
            self.actf(sq.h[:], sl.h[:], AF.Square, [sl.d], [sq.d])
            yield
            p4 = self.bank(hold=True)
            for t in range(4):
                self.mm(p4.h[:, t:t + 1], sq.h[:, t * 128:(t + 1) * 128], self.onesB[:, 0:1], True, True, [sq.d, self.cstb.d], [p4.d])
            rel(sq)
            yield
            r4 = r4r.next(hold=True)
            if c < 8:
                self.actf(r4.h[:], p4.h[:, 0:4], AF.Identity, [p4.d], [r4.d], bias=self.eps6q.h[:, 0:1], scale=128.0)
            else:
                self.actf(r4.h[:], p4.h[:, 0:4], AF.Identity, [p4.d], [r4.d], bias=self.eps6.h[:, 0:1], scale=1.0)
            rel(p4)
            yield
            self.tt("pool", r4.h[:], r4.h[:], nhalf.h[:], ALU.pow, [r4.d, nhalf.d], [r4.d])
            yield
            rn = rnr.next(hold=True)
            self.tt("dve", rn.h[:].rearrange("p (t i) -> p t i", t=4), self.identF.unsqueeze(1).to_broadcast([128, 4, 128]),
                    r4.h[:].unsqueeze(2).to_broadcast([128, 4, 128]), ALU.mult, [self.cst.d, r4.d], [rn.d])
            rel(r4)
            yield
            pn = self.bank(hold=True)
            self.mm(pn.h[:], self.onesF, rn.h[:], True, True, [self.cst.d, rn.d], [pn.d])
            rel(rn)
            yield
            self.tt("dve", oc.h[:], sl.h[:], pn.h[:], ALU.mult, [sl.d, pn.d], [oc.d])
            rel(pn)
            rn = sl
            rel(sl, rn)
            if c < 8:
                P.dma("pool", self.qT_S[g, :, c * 512:(c + 1) * 512], oc.h[:], reads=[oc.d], writes=[self.sdep("qT", g)])
                rel(oc)
                return
            h = c - 8
            P.dma("pool", self.kT_S[g, :, h * 512:(h + 1) * 512], oc.h[:], reads=[oc.d], writes=[self.sdep("kT", g)])
            yield
            tb = self.bank(hold=True)
            tbb = tb.h[:].bitcast(BF16)
            for t in range(4):
                self.tr(tbb[:, t * 128:(t + 1) * 128], oc.h[:, t * 128:(t + 1) * 128], self.identB, [oc.d, self.cstb.d], [tb.d])
            yield
            self.copy("act", ktokG.h[:, :, h * 128:(h + 1) * 128], tbb[:, 0:512].rearrange("p (t n) -> p t n", t=4), [tb.d], [ktokG.d])
            rel(tb, oc)

        def finalize(g):
            rows = slice(g * 512, (g + 1) * 512)
            P.dma("pool", self.ktok_S[rows, :].rearrange("(t p) n -> p t n", p=128), ktokG.h[:], reads=[ktokG.d],
                  writes=[self.sdep("ktok", g * 4 + t) for t in range(4)])
            P.dma("pool", self.vtok_S[rows, :].rearrange("(t p) n -> p t n", p=128), vtokG.h[:], reads=[vtokG.d],
                  writes=[self.sdep("vtok", g * 4 + t) for t in range(4)])
            yield

        order = []
        for i in range(8):
            order += [8 + i, 16 + 2 * i, 16 + 2 * i + 1]
        for i in range(8):
            order += [i, 32 + 2 * i, 32 + 2 * i + 1]

        def tasks():
            for g in range(NG):
                xT = xTs[g % 2]
                for i, c in enumerate(order):
                    yield chunk(g, c, xT)
                    if i == 0 and g + 1 < NG:
                        yield xload(g + 1, xTs[(g + 1) % 2])
                yield finalize(g)

        for _ in xload(0, xTs[0]):
            pass
        pipeline(tasks(), W)

    def phaseG(self):
        P, NT = self.P, self.NT
        raw = self.sb("bgraw", [128, NT, 16])
        out = self.sb("bgout", [128, NT, 16])
        t1 = self.sb("gt1", [128, NT, 8])
        t2 = self.sb("gt2", [128, NT, 8])
        t3 = self.sb("gt3", [128, NT, 8])
        vb = self.sb("vecG", [128, 16])
        negA = self.sb("negA", [128, 8])
        P.dma("sp", raw.h[:], self.bgr_S.rearrange("p (t c) -> p t c", c=16), writes=[raw.d])
        P.dma("sp", vb.h[:], self.vecs[:, 4352:4368].partition_broadcast(128), writes=[vb.d])
        alog, dtb = vb.h[:, 0:8], vb.h[:, 8:16]
        self.tt("dve", t1.h[:], raw.h[:, :, 8:16], dtb.unsqueeze(1).to_broadcast([128, NT, 8]), ALU.add, [raw.d, vb.d], [t1.d])
        P.add("dve", lambda e: e.tensor_scalar(t2.h[:], t1.h[:], -1.0, None, ALU.mult), [t1.d], [t2.d])
        self.tt("dve", t2.h[:], t2.h[:], t1.h[:], ALU.max, [t1.d, t2.d], [t2.d])
        self.actf(out.h[:, :, 0:8], raw.h[:, :, 0:8], AF.Sigmoid, [raw.d], [out.d])
        self.actf(negA.h[:], alog, AF.Exp, [vb.d], [negA.d])
        self.actf(t3.h[:], t2.h[:], AF.Exp, [t2.d], [t3.d], scale=-1.0)
        self.actf(t3.h[:], t3.h[:], AF.Ln, [t3.d], [t3.d], bias=self.one.h[:, 0:1], scale=1.0)
        P.add("dve", lambda e: e.tensor_scalar(negA.h[:], negA.h[:], -1.0, None, ALU.mult), [negA.d], [negA.d])
        self.stt("dve", t2.h[:], t1.h[:], 0.0, t3.h[:], ALU.max, ALU.add, [t1.d, t3.d, t2.d], [t2.d])
        self.tt("dve", out.h[:, :, 8:16], t2.h[:], negA.h[:].unsqueeze(1).to_broadcast([128, NT, 8]), ALU.mult, [t2.d, negA.d, out.d], [out.d])
        P.dma("pool", self.bg_S.rearrange("p (t c) -> p t c", c=16), out.h[:], reads=[out.d], writes=[Dep()])

    def phaseN(self):
        P, NT = self.P, self.NT
        WN = 3
        kTr = self.ring("kTgN", 2, [128, 8, 512], BF16)
        bgr = self.ring("bgN", WN, [128, 16])
        smr = self.ring("smN", WN, [128, 64])
        f32r = [self.ring(f"nf{i}", WN, [128, 8, 128]) for i in range(10)]
        b16r = [self.ring(f"nb{i}", WN, [128, 8, 128], BF16) for i in range(3)]
        kgroups = {}

        def getk(g):
            if g not in kgroups:
                bq = kTr.next()
                P.dma("sp", bq.h[:], self.kT_S[g].rearrange("p (h n) -> p h n", h=8), reads=[self.sdep("kT", g)], writes=[bq.d])
                kgroups[g] = bq
            return kgroups[g]

        def v4(bk):
            return bk.h[:].rearrange("p (h i) -> p h i", h=4)

        def banks2():
            return [self.bank(hold=True), self.bank(hold=True)]

        def ntile(T):
            g, t = divmod(T, 4)
            cs = slice(t * 128, (t + 1) * 128)
            kTg = getk(g)
            bufs = [r_.next(hold=True) for r_ in [bgr, smr] + f32r + b16r]
            bg, sm, R, ttb, tmp2, Dm, pcA, pcB, ptA, ptB, ttA, ttB_, TtB, DT, gbc = bufs
            tmp1, nbD = R, tmp2
            rows = slice(T * 128, (T + 1) * 128)
            P.dma("sp", bg.h[:], self.bg_S[:, T * 16:(T + 1) * 16], reads=[self.sdep("bg", T)], writes=[bg.d])
            beta = bg.h[:, 0:8]
            graw = bg.h[:, 8:16]
            b0 = self.bank(hold=True)
            self.mm(b0.h[:, 0:8], self.Umat, graw, True, True, [self.cst.d, bg.d], [b0.d])
            self.tt("dve", R.h[:], self.Umat.unsqueeze(1).to_broadcast([128, 8, 128]),
                    graw.unsqueeze(2).to_broadcast([128, 8, 128]), ALU.mult, [self.cst.d, bg.d], [R.d])
            yield
            self.copy("dve", sm.h[:, 0:8], b0.h[:, 0:8], [b0.d], [sm.d])
            rel(b0)
            bAB = banks2()
            self.mm(bAB[0].h[:], self.onesF, R.h[:, 0:4, :].rearrange("p h i -> p (h i)"), True, True, [self.cst.d, R.d], [bAB[0].d])
            self.mm(bAB[1].h[:], self.onesF, R.h[:, 4:8, :].rearrange("p h i -> p (h i)"), True, True, [self.cst.d, R.d], [bAB[1].d])
            yield
            for hf, bk in enumerate(bAB):
                hs = slice(hf * 4, hf * 4 + 4)
                self.tt("dve", ttb.h[:, hs, :], v4(bk), sm.h[:, hf * 4:hf * 4 + 4].unsqueeze(2).to_broadcast([128, 4, 128]),
                        ALU.subtract, [bk.d, sm.d], [ttb.d])
                self.actf(gbc.h[:, hs, :], v4(bk), AF.Exp, [bk.d], [gbc.d])
                self.actf(sm.h[:, 48 + hf * 4:52 + hf * 4], v4(bk)[:, :, 127], AF.Exp, [bk.d, sm.d], [sm.d])
            rel(*bAB)
            yield
            self.tt("dve", tmp1.h[:], ttb.h[:], self.NEGU.unsqueeze(1).to_broadcast([128, 8, 128]), ALU.add,
                    [ttb.d, self.cst.d], [tmp1.d])
            self.stt("dve", tmp2.h[:], ttb.h[:], -1.0, self.NEGL.unsqueeze(1).to_broadcast([128, 8, 128]), ALU.mult, ALU.add,
                     [ttb.d, self.cst.d], [tmp2.d])
            yield
            self.actf(DT.h[:], tmp1.h[:], AF.Exp, [tmp1.d], [DT.d])
            self.actf(Dm.h[:], tmp2.h[:], AF.Exp, [tmp2.d], [Dm.d])
            self.actf(sm.h[:, 8:16], sm.h[:, 0:8], AF.Exp, [sm.d], [sm.d])
            yield
            self.stt("dve", sm.h[:, 16:24], beta, -1.0, sm.h[:, 8:16], ALU.mult, ALU.mult, [bg.d, sm.d], [sm.d])
            P.add("dve", lambda e: e.tensor_scalar(sm.h[:, 24:32], beta, -1.0, None, ALU.mult), [bg.d, sm.d], [sm.d])
            self.tt("dve", nbD.h[:], Dm.h[:], sm.h[:, 24:32].unsqueeze(2).to_broadcast([128, 8, 128]), ALU.mult, [Dm.d, sm.d], [nbD.d])
            gb = banks2()
            for h in range(8):
                bk = gb[h // 4]
                self.mm(bk.h[:, (h % 4) * 128:(h % 4 + 1) * 128], kTg.h[:, h, cs], kTg.h[:, h, cs], True, True, [kTg.d], [bk.d])
            yield
            pc, pt, Tt = pcA, ptA, ttA
            for hf in range(2):
                hs = slice(hf * 4, hf * 4 + 4)
                self.tt("dve", pc.h[:, hs, :], v4(gb[hf]), nbD.h[:, hs, :], ALU.mult, [gb[hf].d, nbD.d], [pc.d])
            rel(*gb)
            yield
            tb = banks2()
            for h in range(8):
                bk = tb[h // 4]
                self.tr(bk.h[:, (h % 4) * 128:(h % 4 + 1) * 128], pc.h[:, h, :], self.identF, [pc.d, self.cst.d], [bk.d])
            yield
            for hf in range(2):
                hs = slice(hf * 4, hf * 4 + 4)
                self.copy("act", pt.h[:, hs, :], v4(tb[hf]), [tb[hf].d], [pt.d])
                self.tt("dve", Tt.h[:, hs, :], v4(tb[hf]), self.identF.unsqueeze(1).to_broadcast([128, 4, 128]), ALU.add,
                        [tb[hf].d, self.cst.d], [Tt.d])
            rel(*tb)
            yield
            for lvl in range(1, 7):
                pc2 = pcB if pc is pcA else pcA
                pt2 = ptB if pt is ptA else ptA
                Tt2 = ttB_ if Tt is ttA else ttA
                xb = banks2()
                for h in range(8):
                    bk = xb[h // 4]
                    self.mm(bk.h[:, (h % 4) * 128:(h % 4 + 1) * 128], pt.h[:, h, :], pc.h[:, h, :], True, True, [pt.d, pc.d], [bk.d])
                if lvl < 6:
                    yb = banks2()
                    for h in range(8):
                        bk = yb[h // 4]
                        self.mm(bk.h[:, (h % 4) * 128:(h % 4 + 1) * 128], pc.h[:, h, :], pt.h[:, h, :], True, True, [pt.d, pc.d], [bk.d])
                yield
                for hf in range(2):
                    hs = slice(hf * 4, hf * 4 + 4)
                    self.copy("act", pc2.h[:, hs, :], v4(xb[hf]), [xb[hf].d], [pc2.d])
                rel(*xb)
                if lvl < 6:
                    for hf in range(2):
                        hs = slice(hf * 4, hf * 4 + 4)
                        self.copy("dve" if hf else "act", pt2.h[:, hs, :], v4(yb[hf]), [yb[hf].d], [pt2.d])
                    rel(*yb)
                yield
                zb = banks2()
                for h in range(8):
                    bk = zb[h // 4]
                    self.mm(bk.h[:, (h % 4) * 128:(h % 4 + 1) * 128], pc2.h[:, h, :], Tt.h[:, h, :], True, True, [pc2.d, Tt.d], [bk.d])
                yield
                for hf in range(2):
                    hs = slice(hf * 4, hf * 4 + 4)
                    dst = Tt2 if lvl < 6 else TtB
                    self.tt("dve", dst.h[:, hs, :], v4(zb[hf]), Tt.h[:, hs, :], ALU.add, [zb[hf].d, Tt.d], [dst.d])
                rel(*zb)
                pc, pt, Tt = pc2, pt2, Tt2
                yield
            P.dma("pool", self.Tt_S[T].rearrange("p (h n) -> p h n", h=8), TtB.h[:], reads=[TtB.d], writes=[self.sdep("Tt", T)])
            P.dma("pool", self.DT_S[T].rearrange("p (h n) -> p h n", h=8), DT.h[:], reads=[DT.d], writes=[self.sdep("DT", T)])
            P.dma("pool", self.gb_S[T].rearrange("p (h n) -> p h n", h=8), gbc.h[:], reads=[gbc.d], writes=[self.sdep("gb", T)])
            P.dma("pool", self.sm_S[T], sm.h[:], reads=[sm.d], writes=[self.sdep("sm", T)])
            rel(*bufs)

        pipeline((ntile(T) for T in range(NT)), WN)

    def phaseB(self):
        P, NG, NT = self.P, self.NG, self.NT
        wo = self.sb("woB", [128, 16, 1024], BF16)
        stg = self.ring("wstgB", 2, [128, 16, 128])
        wod = [Dep() for _ in range(8)]
        self.load_weight(wo, wod, self.w_out, 1024, stg, blk=128)
        lnv = self.lnv = self.sb("lnvB", [128, 2048])
        P.dma("sp", lnv.h[:], self.vecs[:, 0:2048].partition_broadcast(128), writes=[lnv.d])
        ngv = self.vb = self.sb("ngvB", [128, 256])
        P.dma("sp", ngv.h[:], self.vecs[:, 4096:4352].partition_broadcast(128), writes=[ngv.d])
        self.normg = ngv.h[:]
        Sf = [self.sb(f"Sf{h}", [128, 256]) for h in range(8)]
        Sb_ = [self.sb(f"Sb{h}", [128, 256], BF16) for h in range(8)]
        for h in range(8):
            P.add("pool", lambda e, h=h: e.memset(Sf[h].h[:], 0.0), [], [Sf[h].d])
            P.add("pool", lambda e, h=h: e.memset(Sb_[h].h[:], 0.0), [], [Sb_[h].d])
        qTr = self.ring("qTgB", 2, [128, 8, 512], BF16)
        kTr = self.ring("kTgB", 2, [128, 8, 512], BF16)
        ktr = self.ring("ktokB", 2, [128, 8, 128], BF16)
        vtr = self.ring("vtokB", 2, [128, 8, 256], BF16)
        bgr = self.ring("bgB", 2, [128, 16])
        DTr = self.ring("DTB", 2, [128, 8, 128], BF16)
        gbr = self.ring("gbB", 2, [128, 8, 128], BF16)
        smr = self.ring("smB", 3, [128, 64])
        TtBr = self.ring("TtB", 2, [128, 8, 128], BF16)
        qgr = self.ring("qg", 2, [128, 8, 128], BF16)
        PTbr = self.ring("PTb", 2, [128, 8, 128], BF16)
        bvr = self.ring("bv", 2, [128, 8, 256], BF16)
        kgbr = self.ring("kgb", 2, [128, 8, 128], BF16)
        osbr = self.ring("osb", 2, [128, 8, 256], BF16)
        zTr = self.ring("zTgB", 2, [128, 16, 128], BF16)
        xfr = self.ring("xfB", 2, [128, 1024])
        onbr = self.ring("onb", 2, [128, 16, 128], BF16)
        ogTr = self.ring("ogT", 2, [128, 16, 128], BF16)
        hpr = self.ring("hpB", 2, [128, 1024])
        s2r = self.ring("s2B", 2, [128, 16])
        rbr = self.ring("rb", 4, [128, 256], BF16)
        vnr = self.ring("vnb", 4, [128, 256], BF16)
        junk = self.sb("junk", [128, 256])
        groups = {}

        def getgroup(g):
            if g not in groups:
                qTg, kTg = qTr.next(), kTr.next()
                P.dma("sp", qTg.h[:], self.qT_S[g].rearrange("p (h n) -> p h n", h=8), reads=[self.sdep("qT", g)], writes=[qTg.d])
                P.dma("sp", kTg.h[:], self.kT_S[g].rearrange("p (h n) -> p h n", h=8), reads=[self.sdep("kT", g)], writes=[kTg.d])
                groups[g] = (qTg, kTg)
            return groups[g]

        def v4(bk):
            return bk.h[:].rearrange("p (h i) -> p h i", h=4)

        recur_done = [False] * NT

        def pipeline_gen(tasks, width):
            tasks = iter(tasks)
            active = []
            done = False
            while True:
                if len(active) < width and not done:
                    try:
                        active.append(next(tasks))
                    except StopIteration:
                        done = True
                if not active:
                    if done:
                        return
                    continue
                for gq in list(active):
                    try:
                        next(gq)
                    except StopIteration:
                        active.remove(gq)
                yield "s"

        def tile_gen(T):
            g, t = divmod(T, 4)
            cs = slice(t * 128, (t + 1) * 128)
            rows = slice(T * 128, (T + 1) * 128)
            qTg, kTg = getgroup(g)
            pre = [r_.next(hold=True) for r_ in (ktr, vtr, bgr, DTr, gbr)]
            kt, vt, bg, DT, gbc = pre
            mid = [r_.next(hold=True) for r_ in (TtBr, qgr, PTbr, bvr, kgbr)]
            TtB, qg, PTb, bv, kgb = mid
            sm = smr.next(hold=True)
            ssd = Dep()
            P.dma("sp", kt.h[:], self.ktok_S[rows, :].rearrange("p (h n) -> p h n", h=8), reads=[self.sdep("ktok", T)], writes=[kt.d])
            P.dma("sp", vt.h[:], self.vtok_S[rows, :].rearrange("p (h n) -> p h n", h=8), reads=[self.sdep("vtok", T)], writes=[vt.d])
            P.dma("sp", bg.h[:], self.bg_S[:, T * 16:(T + 1) * 16], reads=[self.sdep("bg", T)], writes=[bg.d])
            P.dma("sp", sm.h[:], self.sm_S[T], reads=[self.sdep("sm", T)], writes=[sm.d])
            P.dma("sp", TtB.h[:], self.Tt_S[T].rearrange("p (h n) -> p h n", h=8), reads=[self.sdep("Tt", T)], writes=[TtB.d])
            P.dma("sp", DT.h[:], self.DT_S[T].rearrange("p (h n) -> p h n", h=8), reads=[self.sdep("DT", T)], writes=[DT.d])
            P.dma("sp", gbc.h[:], self.gb_S[T].rearrange("p (h n) -> p h n", h=8), reads=[self.sdep("gb", T)], writes=[gbc.d])
            beta = bg.h[:, 0:8]
            yield "s"
            self.tt("dve", qg.h[:], qTg.h[:, :, cs], gbc.h[:], ALU.mult, [qTg.d, gbc.d], [qg.d])
            pb2 = [self.bank(hold=True), self.bank(hold=True)]
            for h in range(8):
                bk = pb2[h // 4]
                self.mm(bk.h[:, (h % 4) * 128:(h % 4 + 1) * 128], kTg.h[:, h, cs], qTg.h[:, h, cs], True, True, [kTg.d, qTg.d], [bk.d])
            yield "s"
            for hf in range(2):
                hs = slice(hf * 4, hf * 4 + 4)
                self.tt("dve", PTb.h[:, hs, :], v4(pb2[hf]), DT.h[:, hs, :], ALU.mult, [pb2[hf].d, DT.d], [PTb.d])
            rel(*pb2)
            yield "s"
            self.tt("dve", bv.h[:], vt.h[:], beta.unsqueeze(2).to_broadcast([128, 8, 256]), ALU.mult, [vt.d, bg.d], [bv.d])
            self.tt("dve", kgb.h[:], kt.h[:], DT.h[:, :, 127:128].to_broadcast([128, 8, 128]), ALU.mult, [kt.d, DT.d], [kgb.d])
            P.add("pool", lambda e: e.memset(sm.h[:, 32:40], 0.0), [sm.d], [ssd])
            rel(*pre)
            while T > 0 and not recur_done[T - 1]:
                yield "s"
            yield "recur"
            osb = osbr.next(hold=True)

            def head(h):
                b1 = self.bank(hold=True)
                self.mm(b1.h[:, 0:256], kTg.h[:, h, cs], Sb_[h].h[:], True, True, [kTg.d, Sb_[h].d], [b1.d])
                yield
                rb = rbr.next(hold=True)
                self.stt("dve", rb.h[:], b1.h[:, 0:256], sm.h[:, 16 + h:17 + h], bv.h[:, h, :], ALU.mult, ALU.add,
                         [b1.d, sm.d, bv.d], [rb.d])
                rel(b1)
                yield
                b2 = self.bank(hold=True)
                self.mm(b2.h[:, 0:256], TtB.h[:, h, :], rb.h[:], True, True, [TtB.d, rb.d], [b2.d])
                rel(rb)
                yield
                vnb = vnr.next(hold=True)
                self.copy("act", vnb.h[:], b2.h[:, 0:256], [b2.d], [vnb.d])
                rel(b2)
                yield
                b3, b4 = self.bank(hold=True), self.bank(hold=True)
                self.mm(b3.h[:, 0:256], qg.h[:, h, :], Sb_[h].h[:], True, False, [qg.d, Sb_[h].d], [b3.d])
                self.mm(b3.h[:, 0:256], PTb.h[:, h, :], vnb.h[:], False, True, [PTb.d, vnb.d], [b3.d])
                self.mm(b4.h[:, 0:256], kgb.h[:, h, :], vnb.h[:], True, True, [kgb.d, vnb.d], [b4.d])
                rel(vnb)
                yield
                self.stt("dve", Sf[h].h[:], Sf[h].h[:], sm.h[:, 48 + h:49 + h], b4.h[:, 0:256], ALU.mult, ALU.add,
                         [Sf[h].d, sm.d, b4.d], [Sf[h].d])
                rel(b4)
                self.actf(junk.h[:], b3.h[:, 0:256], AF.Square, [b3.d], [junk.d, ssd], accum_out=sm.h[:, 32 + h:33 + h])
                self.copy("act", osb.h[:, h, :], b3.h[:, 0:256], [b3.d], [osb.d])
                rel(b3)
                yield
                self.copy("act", Sb_[h].h[:], Sf[h].h[:], [Sf[h].d], [Sb_[h].d])

            yield from pipeline_gen((head(h) for h in range(8)), 3)
            recur_done[T] = True
            rel(*mid)
            yield "post"
            post = [r_.next(hold=True) for r_ in (xfr, zTr, onbr, ogTr, hpr, s2r)]
            xf, zTg, onb, ogT, hp, s2 = post
            P.dma("sp", xf.h[:], self.x[rows, :], writes=[xf.d])
            P.dma("sp", zTg.h[:], self.zT_S[g].rearrange("p (h n) -> p h n", h=16)[:, :, cs], reads=[self.sdep("zT", g)], writes=[zTg.d])
            self.actf(sm.h[:, 40:48], sm.h[:, 32:40], AF.Sqrt, [ssd], [ssd], bias=self.eps6.h[:, 0:1], scale=1.0 / 256.0)
            yield "s"
            P.add("dve", lambda e: e.reciprocal(sm.h[:, 40:48], sm.h[:, 40:48]), [ssd], [ssd])
            for h in range(8):
                self.stt("dve", onb.h[:, 2 * h:2 * h + 2, :].rearrange("p a n -> p (a n)"), osb.h[:, h, :], sm.h[:, 40 + h:41 + h], self.normg,
                         ALU.mult, ALU.mult, [osb.d, ssd, self.vb.d], [onb.d])
            rel(osb)
            yield "s"
            for hf in range(2):
                bk = self.bank(hold=True)
                bkb = bk.h[:].bitcast(BF16)
                for cc in range(8):
                    c = hf * 8 + cc
                    self.tr(bkb[:, cc * 128:(cc + 1) * 128], onb.h[:, c, :], self.identB, [onb.d, self.cstb.d], [bk.d])
                yield "s"
                self.tt("dve", ogT.h[:, hf * 8:hf * 8 + 8, :], bkb.rearrange("p (c n) -> p c n", c=8), zTg.h[:, hf * 8:hf * 8 + 8, :],
                        ALU.mult, [bk.d, zTg.d], [ogT.d])
                rel(bk)
            yield "s"
            yb = [self.bank(hold=True), self.bank(hold=True)]
            for half in range(2):
                for c in range(16):
                    self.mm(yb[half].h[:], ogT.h[:, c, :], wo.h[:, c, half * 512:(half + 1) * 512], c == 0, c == 15,
                            [ogT.d] + wod[half * 4:half * 4 + 4], [yb[half].d])
            yield "s"
            for _ in self.resid_ln_g(0, xf, yb, self.h1_S[rows, :], self.sdep(id(self.h1_S), T), hp, s2):
                yield "s"
            rel(sm, *post)

        active = [[0, tile_gen(0)]]
        nxt = 1
        while active:
            for ent in list(active):
                try:
                    tag = next(ent[1])
                except StopIteration:
                    active.remove(ent)
                    continue
                if tag == "recur" and nxt < NT and ent[0] == nxt - 1:
                    active.append([nxt, tile_gen(nxt)])
                    nxt += 1

    def phaseC(self):
        P, NG = self.P, self.NG
        wC = self.sb("wC", [128, 8, 5120], BF16)
        wd = [Dep() for _ in range(10)]
        stg = self.ring("wstgC", 2, [128, 8, 256])
        wd = [Dep() for _ in range(20)]
        self.load_weight(wC, wd[0:12], self.kv_w, 3072, stg, blk=256)

        class _Off:
            pass
        nbk = 8
        for cbk in range(nbk):
            c0 = cbk * 256
            s = stg.next()
            P.dma("sp", s.h[:], self.aw_in[:, c0:c0 + 256].rearrange("(k p) n -> p k n", p=128), writes=[s.d])
            self.copy(("dve", "act", "pool")[cbk % 3], wC.h[:, :, 3072 + c0:3072 + c0 + 256], s.h[:], [s.d], [wd[12 + cbk]])
        WC = 6
        xfr = self.ring("xfC", 2, [128, 1024])
        hTs = [self.sb("hTC", [128, 8, 512], BF16, nd=4) for _ in range(2)]
        rps = [self.sb("ropeC", [128, 4, 512]) for _ in range(2)]
        KT1 = self.sb("KTgC", [128, 12, 512], BF16)
        QT1 = self.sb("QTgC", [128, 12, 512], BF16)
        KTs, QTs = [KT1, KT1], [QT1, QT1]
        rawr = self.ring("rawC", WC, [128, 512], BF16)
        t1r = self.ring("t1C", WC, [128, 512])
        t2r = self.ring("t2C", WC, [128, 512])
        vtr = self.ring("vtC", 3, [128, 1536], BF16)
        ztr = self.ring("ztC", 3, [128, 512], BF16)
        done = [0] * NG

        def xl(g):
            self.load_xT(self.h1_S, g, xfr, hTs[g % 2])
            rp = rps[g % 2]
            P.dma("sp", rp.h[:], self.rope[:, :, g * 512:(g + 1) * 512].rearrange("a p n -> p a n"), writes=[rp.d])
            yield

        def head(g, qk, hd):
            hT, rp = hTs[g % 2], rps[g % 2]
            dst = (QTs if qk else KTs)[g % 2]
            col0 = (3072 if qk else 0) + hd * 128
            pb = self.bank(hold=True)
            for k in range(8):
                self.mm(pb.h[:], wC.h[:, k, col0:col0 + 128], hT.h[:, k, :], k == 0, k == 7, [wd[col0 // 256]] + hT.ds, [pb.d])
            yield
            raw, t1 = rawr.next(hold=True), t1r.next(hold=True)
            self.copy("act", raw.h[:], pb.h[:], [pb.d], [raw.d])
            self.tt("dve", t1.h[:], pb.h[:], rp.h[:, 2 * qk, :], ALU.mult, [pb.d, rp.d], [t1.d])
            rel(pb)
            yield
            p2 = self.bank(hold=True)
            self.mm(p2.h[:], self.permB, raw.h[:], True, True, [self.cstb.d, raw.d], [p2.d])
            rel(raw)
            yield
            t2 = t2r.next(hold=True)
            self.tt("dve", t2.h[:], p2.h[:], rp.h[:, 2 * qk + 1, :], ALU.mult, [p2.d, rp.d], [t2.d])
            rel(p2)
            yield
            self.tt("pool", dst.h[:, hd, :], t1.h[:], t2.h[:], ALU.add, [t1.d, t2.d], [dst.d])
            rel(t1, t2)
            done[g] += 1

        def vz(g, t):
            hT = hTs[g % 2]
            T = g * 4 + t
            rows = slice(T * 128, (T + 1) * 128)
            pbs = [self.bank(hold=True) for _ in range(2)]
            for cg in range(2):
                c0 = 1536 + cg * 512
                for k in range(8):
                    self.mm(pbs[cg].h[:], hT.h[:, k, t * 128:(t + 1) * 128], wC.h[:, k, c0:c0 + 512], k == 0, k == 7, [hT.ds[t], wd[c0 // 256], wd[c0 // 256 + 1]], [pbs[cg].d])
            yield
            vt = vtr.next(hold=True)
            for cg in range(2):
                self.copy("act" if cg % 2 else "dve", vt.h[:, cg * 512:(cg + 1) * 512], pbs[cg].h[:], [pbs[cg].d], [vt.d])
            rel(*pbs)
            yield
            pbs = [self.bank(hold=True) for _ in range(2)]
            for k in range(8):
                self.mm(pbs[0].h[:], hT.h[:, k, t * 128:(t + 1) * 128], wC.h[:, k, 2560:3072], k == 0, k == 7, [hT.ds[t], wd[10], wd[11]], [pbs[0].d])
            for k in range(8):
                self.mm(pbs[1].h[:], hT.h[:, k, t * 128:(t + 1) * 128], wC.h[:, k, 4608:5120], k == 0, k == 7, [hT.ds[t], wd[18], wd[19]], [pbs[1].d])
            yield
            zt = ztr.next(hold=True)
            self.copy("dve", vt.h[:, 1024:1536], pbs[0].h[:], [pbs[0].d], [vt.d])
            self.actf(zt.h[:], pbs[1].h[:], AF.Silu, [pbs[1].d], [zt.d])
            rel(*pbs)
            P.dma("pool", self.V_S[rows, :], vt.h[:], reads=[vt.d], writes=[self.sdep("V", T)])
            P.dma("pool", self.Z_S[rows, :], zt.h[:], reads=[zt.d], writes=[self.sdep("Z", T)])
            rel(vt, zt)

        def fin(g):
            while done[g] < 24:
                yield
            P.dma("pool", self.KT_S[g].rearrange("p (h n) -> p h n", h=12), KTs[g % 2].h[:], reads=[KTs[g % 2].d], writes=[self.sdep("KT", g)])
            P.dma("pool", self.QT_S[g].rearrange("p (h n) -> p h n", h=12), QTs[g % 2].h[:], reads=[QTs[g % 2].d], writes=[self.sdep("QT", g)])

        def tasks():
            for g in range(NG):
                lst = [head(g, qk, hd) for qk in range(2) for hd in range(12)]
                for t in range(4):
                    lst.insert(6 * t + 3 + t, vz(g, t))
                for i, tk in enumerate(lst):
                    yield tk
                    if i == 0 and g + 1 < NG:
                        yield xl(g + 1)
                yield fin(g)

        for _ in xl(0):
            pass
        pipeline(tasks(), WC)

    def phaseD(self):
        P, S = self.P, self.S
        for gi, d in enumerate((1, 4, 16)):
            if gi > 0:
                self.new_phase()
            WD = 8
            smr = self.ring("smD", 4, [128, 256])
            pbr = self.ring("pbD", 4, [128, 256], BF16)
            pTr = self.ring("pTD", 4, [128, 256], BF16)
            osr = self.ring("osD", 4, [128, 4, 128])
            sr = self.ring("sD", 4, [128, 16], nd=4)
            W = max(512, 128 * d)
            ngrp = W // 512
            nbuf = S // W
            spb = W // (128 * d)
            Qr = self.ring(f"QD{gi}", 3, [128, 4, W], BF16)
            Kr = self.ring(f"KD{gi}", 4, [128, 4, W], BF16)
            Vr = self.ring(f"VD{gi}", 4, [128, W // 128, 512], BF16)
            bufs = {}

            def getbuf(bi, Qr=Qr, Kr=Kr, Vr=Vr, bufs=bufs, ngrp=ngrp, spb=spb, W=W, d=d, gi=gi):
                if bi in bufs:
                    return bufs[bi]
                q, k, v = Qr.next(), Kr.next(), Vr.next()
                for j in range(ngrp):
                    gg = bi * ngrp + j
                    P.dma("sp", q.h[:, :, j * 512:(j + 1) * 512],
                          self.QT_S[gg].rearrange("p (h n) -> p h n", h=12)[:, gi * 4:gi * 4 + 4, :], reads=[self.sdep("QT", gg)], writes=[q.d])
                    P.dma("sp", k.h[:, :, j * 512:(j + 1) * 512],
                          self.KT_S[gg].rearrange("p (h n) -> p h n", h=12)[:, gi * 4:gi * 4 + 4, :], reads=[self.sdep("KT", gg)], writes=[k.d])
                for sp in range(spb):
                    base = bi * W + sp * 128 * d
                    rd = [self.sdep("V", T) for T in range(base // 128, (base + 128 * d) // 128)]
                    P.dma("sp", v.h[:, sp * d:(sp + 1) * d, :],
                          self.V_S[base:base + 128 * d, gi * 512:(gi + 1) * 512].rearrange("(j r) c -> j r c", r=d), reads=rd, writes=[v.d])
                bufs[bi] = (q, k, v)
                return bufs[bi]

            nspan = S // (128 * d)

            class Blk:
                pass

            def head(bl, hh, d=d):
                n, r, sm, osb = bl.n, bl.r, bl.sm, bl.osb
                q, kc, vc, off, so = bl.q, bl.kc, bl.vc, bl.off, bl.so
                lo = 0 if n > 0 else 128
                sd = sm.ds[hh]
                P.add("pool", lambda e: e.memset(sm.h[:, hh:hh + 1], 0.0), [sd], [sd])
                sc = self.bank(hold=True)
                qs = q.h[:, hh, off + r:off + 128 * d:d]
                if n > 0:
                    self.mm(sc.h[:, 0:128], qs, bl.kp.h[:, hh, bl.poff + r:bl.poff + 128 * d:d], True, True, [q.d, bl.kp.d], [sc.d])
                self.mm(sc.h[:, 128:256], qs, kc.h[:, hh, off + r:off + 128 * d:d], True, True, [q.d, kc.d], [sc.d])
                yield
                smb = smr.next(hold=True)
                self.tt("dve", smb.h[:, lo:256], sc.h[:, lo:256], self.maskA[:, lo:256], ALU.add, [sc.d, self.cst.d], [smb.d])
                rel(sc)
                P.add("dve", lambda e: e.tensor_reduce(sm.h[:, 4 + hh:5 + hh], smb.h[:, lo:256], AX.X, ALU.max, negate=True), [smb.d, sd], [sd])
                yield
                pbf = pbr.next(hold=True)
                self.actf(pbf.h[:, lo:256], smb.h[:, lo:256], AF.Exp, [smb.d, sd], [pbf.d, sd],
                          bias=sm.h[:, 4 + hh:5 + hh], scale=1.0, accum_out=sm.h[:, hh:hh + 1])
                rel(smb)
                yield
                tbk = self.bank(hold=True)
                tbb = tbk.h[:].bitcast(BF16)
                if n > 0:
                    self.tr(tbb[:, 0:128], pbf.h[:, 0:128], self.identB, [pbf.d, self.cstb.d], [tbk.d])
                self.tr(tbb[:, 128:256], pbf.h[:, 128:256], self.identB, [pbf.d, self.cstb.d], [tbk.d])
                rel(pbf)
                yield
                pT = pTr.next(hold=True)
                self.copy("act", pT.h[:, lo:256], tbb[:, lo:256], [tbk.d], [pT.d])
                rel(tbk)
                P.add("dve", lambda e: e.reciprocal(sm.h[:, 8 + hh:9 + hh], sm.h[:, hh:hh + 1]), [sd], [sd])
                yield
                ob = self.bank(hold=True)
                if n > 0:
                    self.mm(ob.h[:, 0:128], pT.h[:, 0:128], bl.vp.h[:, bl.pso * d + r, hh * 128:(hh + 1) * 128], True, False, [pT.d, bl.vp.d], [ob.d])
                self.mm(ob.h[:, 0:128], pT.h[:, 128:256], vc.h[:, so * d + r, hh * 128:(hh + 1) * 128], n == 0, True, [pT.d, vc.d], [ob.d])
                rel(pT)
                yield
                P.add("dve", lambda e: e.tensor_scalar(osb.h[:, hh, :], ob.h[:, 0:128], sm.h[:, 8 + hh:9 + hh], None, ALU.mult),
                      [ob.d, sd], [osb.d])
                rel(ob)
                bl.done += 1

            def fin(bl, d=d, gi=gi):
                while bl.done < 4:
                    yield
                sm, osb, r = bl.sm, bl.osb, bl.r
                self.actf(sm.h[:, 12:16], sm.h[:, 0:4], AF.Ln, sm.ds, sm.ds)
                yield
                self.tt("dve", sm.h[:, 12:16], sm.h[:, 12:16], sm.h[:, 4:8], ALU.subtract, sm.ds, sm.ds)
                base = bl.n * 128 * d
                tl = list(range(base // 128, (base + 128 * d) // 128))
                P.dma("pool", self.O_S[gi, base:base + 128 * d, :].rearrange("(i r) c -> i r c", r=d)[:, r, :],
                      osb.h[:].rearrange("p h n -> p (h n)"), reads=[osb.d], writes=[self.sdep("O", gi, T, r) for T in tl])
                P.dma("pool", self.L_S[gi, base:base + 128 * d, :].rearrange("(i r) c -> i r c", r=d)[:, r, :],
                      sm.h[:, 12:16], reads=sm.ds, writes=[self.sdep("L", gi, T, r) for T in tl])
                rel(sm, osb)

            def tasks(d=d, spb=spb, nspan=nspan, nbuf=nbuf):
                for n in range(nspan):
                    bi, so = n // spb, n % spb
                    q, kc, vc = getbuf(bi)
                    if so == 0 and bi + 1 < nbuf:
                        getbuf(bi + 1)
                    for r in range(d):
                        bl = Blk()
                        bl.n, bl.r, bl.q, bl.kc, bl.vc, bl.so, bl.off, bl.done = n, r, q, kc, vc, so, so * 128 * d, 0
                        if n > 0:
                            pbi, bl.pso = (n - 1) // spb, (n - 1) % spb
                            _, bl.kp, bl.vp = getbuf(pbi)
                            bl.poff = bl.pso * 128 * d
                        bl.osb, bl.sm = osr.next(hold=True), sr.next(hold=True)
                        for hh in range(4):
                            yield head(bl, hh)
                        yield fin(bl)

            pipeline(tasks(), WD)

    def phaseE(self):
        P, NT = self.P, self.NT
        wo = self.sb("woE", [128, 4, 1024], BF16)
        wod = [Dep() for _ in range(2)]
        stg = self.ring("wstgE", 2, [128, 4, 512])
        self.load_weight(wo, wod, self.aw_out, 1024, stg)
        lnv = self.lnv = self.sb("lnvE", [128, 2048])
        P.dma("sp", lnv.h[:], self.vecs[:, 2048:4096].partition_broadcast(128), writes=[lnv.d])
        WE = 4
        Or = self.ring("OE", WE + 1, [128, 3, 512])
        Lr = self.ring("LE", WE + 1, [128, 3, 4])
        zr = self.ring("zE", WE + 1, [128, 512], BF16)
        xfr = self.ring("xfE", WE + 1, [128, 1024])
        smr = self.ring("smE", WE + 1, [128, 32])
        tmr = self.ring("tmE", WE + 1, [128, 3, 512])
        ogr = self.ring("ogE", WE + 1, [128, 512], BF16)
        oTr = self.ring("oTE", WE + 1, [128, 4, 128], BF16)
        hpr = self.ring("hpE", WE + 1, [128, 1024])
        s2r = self.ring("s2E", WE + 1, [128, 16])

        def tile(T):
            rows = slice(T * 128, (T + 1) * 128)
            bufs = [r_.next(hold=True) for r_ in (Or, Lr, zr, xfr, smr, tmr, ogr, oTr, hpr, s2r)]
            O, L, z, xf, sm, tm, og, oT, hp, s2 = bufs
            for gi, d in enumerate((1, 4, 16)):
                P.dma("sp", O.h[:, gi, :], self.O_S[gi, rows, :], reads=[self.sdep("O", gi, T, r) for r in range(d)], writes=[O.d])
                P.dma("sp", L.h[:, gi, :], self.L_S[gi, rows, :], reads=[self.sdep("L", gi, T, r) for r in range(d)], writes=[L.d])
            P.dma("sp", z.h[:], self.Z_S[rows, :], reads=[self.sdep("Z", T)], writes=[z.d])
            P.dma("sp", xf.h[:], self.h1_S[rows, :], reads=[self.sdep(id(self.h1_S), T)], writes=[xf.d])
            yield
            self.tt("dve", sm.h[:, 0:4], L.h[:, 0, :], L.h[:, 1, :], ALU.max, [L.d], [sm.d])
            self.tt("dve", sm.h[:, 0:4], sm.h[:, 0:4], L.h[:, 2, :], ALU.max, [L.d, sm.d], [sm.d])
            e3 = sm.h[:, 4:16].rearrange("p (g h) -> p g h", g=3)
            self.tt("dve", e3, L.h[:], sm.h[:, 0:4].unsqueeze(1).to_broadcast([128, 3, 4]), ALU.subtract, [L.d, sm.d], [sm.d])
            yield
            self.actf(sm.h[:, 4:16], sm.h[:, 4:16], AF.Exp, [sm.d], [sm.d])
            yield
            self.tt("dve", sm.h[:, 16:20], sm.h[:, 4:8], sm.h[:, 8:12], ALU.add, [sm.d], [sm.d])
            self.tt("dve", sm.h[:, 16:20], sm.h[:, 16:20], sm.h[:, 12:16], ALU.add, [sm.d], [sm.d])
            P.add("dve", lambda e: e.reciprocal(sm.h[:, 20:24], sm.h[:, 16:20]), [sm.d], [sm.d])
            self.tt("dve", e3, e3, sm.h[:, 20:24].unsqueeze(1).to_broadcast([128, 3, 4]), ALU.mult, [sm.d], [sm.d])
            yield
            for gi in range(3):
                self.tt("dve", tm.h[:, gi, :].rearrange("p (h n) -> p h n", h=4), O.h[:, gi, :].rearrange("p (h n) -> p h n", h=4),
                        sm.h[:, 4 + gi * 4:8 + gi * 4].unsqueeze(2).to_broadcast([128, 4, 128]), ALU.mult, [O.d, sm.d], [tm.d])
            yield
            self.tt("pool", tm.h[:, 0, :], tm.h[:, 0, :], tm.h[:, 1, :], ALU.add, [tm.d], [tm.d])
            self.tt("pool", tm.h[:, 0, :], tm.h[:, 0, :], tm.h[:, 2, :], ALU.add, [tm.d], [tm.d])
            yield
            self.tt("dve", og.h[:], tm.h[:, 0, :], z.h[:], ALU.mult, [tm.d, z.d], [og.d])
            yield
            bk = self.bank(hold=True)
            bkb = bk.h[:].bitcast(BF16)
            for c in range(4):
                self.tr(bkb[:, c * 128:(c + 1) * 128], og.h[:, c * 128:(c + 1) * 128], self.identB, [og.d, self.cstb.d], [bk.d])
            yield
            self.copy("act", oT.h[:], bkb[:, 0:512].rearrange("p (c n) -> p c n", c=4), [bk.d], [oT.d])
            rel(bk)
            yield
            yb = [self.bank(hold=True), self.bank(hold=True)]
            for half in range(2):
                for c in range(4):
                    self.mm(yb[half].h[:], oT.h[:, c, :], wo.h[:, c, half * 512:(half + 1) * 512], c == 0, c == 3, [oT.d, wod[half]], [yb[half].d])
            yield
            yield from self.resid_ln_g(1, xf, yb, self.out[rows, :], Dep(), hp, s2)
            rel(*bufs)

        pipeline((tile(T) for T in range(NT)), WE)

    def small_consts(self):
        P = self.P
        for name, val in (("eps5", 1e-5), ("eps6", 1e-6), ("eps6q", 128e-6), ("one", 1.0)):
            b = self.sb(name, [128, 1])
            P.add("pool", lambda e, b=b, val=val: e.memset(b.h[:], val), [], [b.d])
            setattr(self, name, b)

    def build(self, phases="AGNBCDE"):
        self.load_consts()
        self.small_consts()
        self.sb_glob = self.sb_ptr
        for ph in phases:
            self.new_phase()
            getattr(self, "phase" + ph)()
            self.sb_use = getattr(self, "sb_use", {})
            self.sb_use[ph] = self.sb_ptr
        self.P.emit()
        return self.nc


def host_consts(S):
    p = np.arange(128)[:, None]
    f = np.arange(128)[None, :]
    c = np.zeros((128, 1024), np.float32)
    c[:, 0:128] = (p == f)
    c[:, 128:256] = (p <= f)
    c[:, 256:384] = np.where(f >= p, 0.0, NEG)
    c[:, 384:512] = np.where(f < p, 0.0, NEG)
    c[:, 512:640] = 1.0
    c[:, 640:768] = (p == (f + 64) % 128)
    i = np.arange(128)[:, None]
    j = np.arange(256)[None, :]
    off = i + 128 - j
    c[:, 768:1024] = np.where((off >= 0) & (off <= 128), 0.0, NEG)
    inv = 1.0 / (10000.0 ** (np.arange(0, 128, 2, dtype=np.float32) / 128))
    ang = np.arange(S, dtype=np.float32)[:, None] * inv[None, :].astype(np.float32)
    ang = np.concatenate([ang, ang], -1).astype(np.float32)
    cos = np.cos(ang).T.astype(np.float32)
    sin = np.sin(ang).T.astype(np.float32)
    sgn = np.where(np.arange(128) < 64, -1.0, 1.0).astype(np.float32)[:, None]
    sc = np.float32(128 ** -0.5)
    rope = np.stack([cos, sin * sgn, cos * sc, sin * sgn * sc]).astype(np.float32)
    return c, rope


_CACHE = {}


def make_inputs(S, x, ln_g, ln_b, gdn_w_in, gdn_conv_w, gdn_a_log, gdn_dt_bias, gdn_norm_g, gdn_w_out, kv_w, att_w_in, att_w_out):
    c, rope = host_consts(S)
    vecs = np.concatenate([ln_g[0], ln_b[0], ln_g[1], ln_b[1], gdn_norm_g[0], gdn_a_log[0], gdn_dt_bias[0]]).astype(np.float32)[None, :]
    cw = np.ascontiguousarray(gdn_conv_w[0].reshape(4, 32, 128).transpose(2, 1, 0)).reshape(128, 128)
    shared = dict(w_in=np.ascontiguousarray(gdn_w_in[0]), w_out=np.ascontiguousarray(gdn_w_out[0]), kv_w=np.ascontiguousarray(kv_w),
                  aw_in=np.ascontiguousarray(att_w_in[0]), aw_out=np.ascontiguousarray(att_w_out[0]), cw=cw, vecs=vecs, consts=c, rope=rope)
    return [dict(shared, x=np.ascontiguousarray(x[b])) for b in range(x.shape[0])]


def kernel(**inputs):
    x = np.asarray(inputs["x"], np.float32)
    Bn, S, _ = x.shape
    args = {k: np.asarray(v, np.float32) for k, v in inputs.items()}
    in_maps = make_inputs(S, **args)
    if S not in _CACHE:
        _CACHE[S] = K(S).build()
    nc = _CACHE[S]
    res = run_bass_kernel_spmd(nc, in_maps, core_ids=list(range(Bn)))
    return np.stack([np.asarray(r["out"], np.float32) for r in res.results], 0)
```
